# Optimizing a Trainium2 kernel written in Bass

```python
import math
import jax, jax.numpy as jnp
from jax import lax
import numpy as np

D_MODEL = 4096
BATCH = 4
SEQ = 2048
DEPTH = 4
DEC_BATCH = 8
DEC_SEQ = 1
PAST_LEN = 8192
PAGE_SIZE = 128

N_A_LAYERS = DEPTH // 2
N_B_LAYERS = DEPTH - N_A_LAYERS
D_RNN = D_MODEL
LRU_BLOCKS = 16
LRU_BS = D_RNN // LRU_BLOCKS
CONV_W = 4
LRU_C = 8.0
N_HEADS = 32
HEAD_DIM = D_MODEL // N_HEADS
N_KV = 4
Q_PER_KV = N_HEADS // N_KV
CMP_BLK = 32
CMP_STRIDE = 16
CMP_HID = 2 * HEAD_DIM
SLC_BLK = 64
TOP_N = 16
WINDOW = 512
N_BUCKETS = 32
MAX_DIST = 128
GLOBAL_QBLK = 32
WIN_QBLK = 128
NEG = -1e30
BIG = 1e30
EPS = 1e-6

kernel_name = "yoco_rglru_nsa_decoder_step"


def rms_norm(x, g):
    xf = x.astype(jnp.float32)
    y = xf * lax.rsqrt(jnp.mean(xf * xf, axis=-1, keepdims=True) + EPS)
    return (y * g.astype(jnp.float32)).astype(x.dtype)


def rel_bucket(d):
    d = jnp.maximum(d, 0)
    n_exact = N_BUCKETS // 2
    df = jnp.maximum(d, 1).astype(jnp.float32)
    large = n_exact + (jnp.log(df / n_exact) / math.log(MAX_DIST / n_exact)
                       * (N_BUCKETS - n_exact)).astype(jnp.int32)
    large = jnp.minimum(large, N_BUCKETS - 1)
    return jnp.where(d < n_exact, d, large)


def rglru_layer(x, pos, h0, conv0, norm_g, w_in, conv_w, conv_b, w_rg, b_rg, w_ig, b_ig, lam, w_out):
    B, T, _ = x.shape
    u = rms_norm(x, norm_g) @ w_in
    xb, gate = u[..., :D_RNN], u[..., D_RNN:]
    xpad = jnp.concatenate([conv0.astype(xb.dtype), xb], axis=1)
    xc = conv_b
    for k in range(CONV_W):
        xc = xc + conv_w[k] * xpad[:, k:k + T]
    xblk = xc.reshape(B, T, LRU_BLOCKS, LRU_BS)
    r = jax.nn.sigmoid((jnp.einsum('btnd,nde->btne', xblk, w_rg).reshape(B, T, D_RNN) + b_rg).astype(jnp.float32))
    i = jax.nn.sigmoid((jnp.einsum('btnd,nde->btne', xblk, w_ig).reshape(B, T, D_RNN) + b_ig).astype(jnp.float32))
    log_a = -LRU_C * r * jax.nn.softplus(-lam.astype(jnp.float32))
    a = jnp.exp(log_a)
    mult = jnp.where((pos == 0)[None, :, None], 1.0, jnp.sqrt(-jnp.expm1(2.0 * log_a)))
    b_in = mult * i * xc.astype(jnp.float32)

    def step(h, ab):
        h = ab[0] * h + ab[1]
        return h, h

    h_last, hs = lax.scan(step, h0.astype(jnp.float32), (a.transpose(1, 0, 2), b_in.transpose(1, 0, 2)))
    hs = hs.transpose(1, 0, 2).astype(x.dtype)
    y = (hs * jax.nn.silu(gate)) @ w_out
    return x + y, h_last.astype(h0.dtype), xpad[:, T:].astype(conv0.dtype)


def shared_kv_rows(x, kv_norm, w_kv):
    B, T, _ = x.shape
    kv = (rms_norm(x, kv_norm) @ w_kv).reshape(B, T, 3, 2, N_KV, HEAD_DIM)
    return kv[:, :, 0], kv[:, :, 1], kv[:, :, 2]


def global_context(cmp_rows, slc_rows, k_norm, cmp_pos, w_cmp1, w_cmp2):
    L = cmp_rows.shape[1]
    n_cmp = (L - CMP_BLK) // CMP_STRIDE + 1
    n_slc = -(-L // SLC_BLK)
    idx = jnp.arange(n_cmp)[:, None] * CMP_STRIDE + jnp.arange(CMP_BLK)[None, :]
    blk = cmp_rows[:, idx] + cmp_pos.transpose(1, 0, 2)[None, None, :, :, None, :]
    hid = jax.nn.silu(jnp.einsum('bnlsgd,sldh->bsgnh', blk, w_cmp1))
    comp = jnp.einsum('bsgnh,shd->bsgnd', hid, w_cmp2)
    kc = rms_norm(comp[:, 0], k_norm[0])
    vc = comp[:, 1]

    def to_blocks(v):
        B = v.shape[0]
        v = jnp.pad(v, ((0, 0), (0, n_slc * SLC_BLK - L), (0, 0), (0, 0)))
        return v.reshape(B, n_slc, SLC_BLK, N_KV, HEAD_DIM).transpose(0, 3, 1, 2, 4)

    ks = to_blocks(rms_norm(slc_rows[:, :, 0], k_norm[1]))
    vs = to_blocks(slc_rows[:, :, 1])
    cs = jnp.arange(n_cmp)[:, None] * CMP_STRIDE
    ss = jnp.arange(n_slc)[None, :] * SLC_BLK
    ov = jnp.minimum(cs + CMP_BLK, ss + SLC_BLK) - jnp.maximum(cs, ss)
    m_imp = jnp.maximum(ov, 0).astype(jnp.float32) / CMP_BLK
    return kc, vc, ks, vs, m_imp


def nsa_global(q, q_pos, kc, vc, ks, vs, tbl, m_imp):
    B, G = q.shape[:2]
    n_cmp = kc.shape[2]
    n_slc = ks.shape[2]
    n_sel = min(TOP_N, n_slc)
    d_c = q_pos[:, None] - (jnp.arange(n_cmp) * CMP_STRIDE + (CMP_BLK - 1))[None, :]
    c_ok = d_c >= 0
    bias_c = tbl[:, rel_bucket(d_c)].transpose(0, 3, 1, 2)
    lc = jnp.einsum('bgrqd,bgnd->bgrqn', q, kc).astype(jnp.float32) + bias_c
    pc = jax.nn.softmax(jnp.where(c_ok, lc, NEG), axis=-1) * c_ok
    o_cmp = jnp.einsum('bgrqn,bgnd->bgrqd', pc.astype(vc.dtype), vc)
    imp = jnp.einsum('bgrqn,nj->bgqj', pc, m_imp)
    cur = q_pos // SLC_BLK
    j = jnp.arange(n_slc)[None, :]
    forced = (j == 0) | (j == cur[:, None]) | (j == cur[:, None] - 1)
    allowed = j <= cur[:, None]
    score = jnp.where(forced, BIG, jnp.where(allowed, imp, NEG))
    top_val, top_idx = lax.top_k(score, n_sel)
    blk_ok = top_val > 0.5 * NEG
    bi = jnp.arange(B)[:, None, None, None]
    gi = jnp.arange(G)[None, :, None, None]
    ksel = ks[bi, gi, top_idx]
    vsel = vs[bi, gi, top_idx]
    kpos = top_idx[..., None] * SLC_BLK + jnp.arange(SLC_BLK)
    d_s = q_pos[None, None, :, None, None] - kpos
    s_ok = blk_ok[..., None] & (d_s >= 0)
    bias_s = tbl[gi[..., None], rel_bucket(d_s)].transpose(0, 1, 5, 2, 3, 4)
    ls = jnp.einsum('bgrqd,bgqnkd->bgrqnk', q, ksel).astype(jnp.float32) + bias_s
    ls = jnp.where(s_ok[:, :, None], ls, NEG).reshape(B, G, q.shape[2], q.shape[3], n_sel * SLC_BLK)
    ps = jax.nn.softmax(ls, axis=-1).reshape(B, G, q.shape[2], q.shape[3], n_sel, SLC_BLK)
    o_slc = jnp.einsum('bgrqnk,bgqnkd->bgrqd', ps.astype(vsel.dtype), vsel)
    return o_cmp, o_slc


def window_attn(q, q_pos, kw, vw, k_pos, tbl):
    d = q_pos[:, None] - k_pos[None, :]
    ok = (d >= 0) & (d <= WINDOW) & (k_pos[None, :] >= 0)
    bias = tbl[:, rel_bucket(d)].transpose(0, 3, 1, 2)
    lw = jnp.einsum('bgrqd,bgkd->bgrqk', q, kw).astype(jnp.float32) + bias
    pw = jax.nn.softmax(jnp.where(ok, lw, NEG), axis=-1)
    return jnp.einsum('bgrqk,bgkd->bgrqd', pw.astype(vw.dtype), vw)


def prompt_attend(q, ctx, win_rows, k_norm, tbl):
    kc, vc, ks, vs, m_imp = ctx
    B, G, R, T, HD = q.shape
    pos = jnp.arange(T, dtype=jnp.int32)
    nq = T // GLOBAL_QBLK
    qg = q.reshape(B, G, R, nq, GLOBAL_QBLK, HD).transpose(3, 0, 1, 2, 4, 5)
    o_cmp, o_slc = lax.map(lambda a: nsa_global(a[0], a[1], kc, vc, ks, vs, tbl, m_imp),
                           (qg, pos.reshape(nq, GLOBAL_QBLK)))
    o_cmp = o_cmp.transpose(1, 2, 3, 0, 4, 5).reshape(B, G, R, T, HD)
    o_slc = o_slc.transpose(1, 2, 3, 0, 4, 5).reshape(B, G, R, T, HD)
    pad = ((0, 0), (0, 0), (WINDOW, 0), (0, 0))
    kw_pad = jnp.pad(rms_norm(win_rows[:, :, 0], k_norm[2]).transpose(0, 2, 1, 3), pad)
    vw_pad = jnp.pad(win_rows[:, :, 1].transpose(0, 2, 1, 3), pad)
    nw = T // WIN_QBLK
    qw = q.reshape(B, G, R, nw, WIN_QBLK, HD).transpose(3, 0, 1, 2, 4, 5)

    def win_block(a):
        qb, c = a
        start = c * WIN_QBLK
        kb = lax.dynamic_slice_in_dim(kw_pad, start, WINDOW + WIN_QBLK, axis=2)
        vb = lax.dynamic_slice_in_dim(vw_pad, start, WINDOW + WIN_QBLK, axis=2)
        k_pos = start - WINDOW + jnp.arange(WINDOW + WIN_QBLK, dtype=jnp.int32)
        q_pos = start + jnp.arange(WIN_QBLK, dtype=jnp.int32)
        return window_attn(qb, q_pos, kb, vb, k_pos, tbl)

    o_win = lax.map(win_block, (qw, jnp.arange(nw, dtype=jnp.int32)))
    o_win = o_win.transpose(1, 2, 3, 0, 4, 5).reshape(B, G, R, T, HD)
    return o_cmp, o_slc, o_win


def sample_attend(q, q_pos, ctx, full_win, win_pos, k_norm, tbl):
    kc, vc, ks, vs, m_imp = ctx
    o_cmp, o_slc = nsa_global(q, q_pos, kc, vc, ks, vs, tbl, m_imp)
    kw = rms_norm(full_win[:, :, 0], k_norm[2]).transpose(0, 2, 1, 3)
    vw = full_win[:, :, 1].transpose(0, 2, 1, 3)
    o_win = window_attn(q, q_pos, kw, vw, win_pos, tbl)
    return o_cmp, o_slc, o_win


def nsa_layer(x, attend, norm_g, w_in, gate_bias, q_norm, w_out):
    B, T, _ = x.shape
    HQ = N_HEADS * HEAD_DIM
    u = rms_norm(x, norm_g) @ w_in
    q = rms_norm(u[..., :HQ].reshape(B, T, N_KV, Q_PER_KV, HEAD_DIM), q_norm) * HEAD_DIM ** -0.5
    q = q.transpose(0, 2, 3, 1, 4)
    gate = u[..., HQ:2 * HQ]
    bg = jax.nn.sigmoid((u[..., 2 * HQ:] + gate_bias).astype(jnp.float32))
    bg = bg.reshape(B, T, N_KV, Q_PER_KV, 3).transpose(0, 2, 3, 1, 4)
    o_cmp, o_slc, o_win = attend(q)
    o = bg[..., 0:1] * o_cmp + bg[..., 1:2] * o_slc + bg[..., 2:3] * o_win
    o = o.transpose(0, 3, 1, 2, 4).reshape(B, T, HQ).astype(x.dtype)
    return x + (o * jax.nn.silu(gate)) @ w_out


def setup_inputs(seed: int = 0) -> dict:
    key = jax.random.key(seed)
    ks = jax.random.split(key, 32)
    f = jnp.float32
    n_pages = PAST_LEN // PAGE_SIZE
    n_used = DEC_BATCH * n_pages
    n_pool = (5 * n_used + 3) // 4
    wb = min(WINDOW, PAST_LEN)
    A, Bn = N_A_LAYERS, N_B_LAYERS

    def nrm(k, shape, scale):
        return jax.random.normal(k, shape, f) * scale

    a0 = jax.random.uniform(ks[16], (A, D_RNN), f, 0.9, 0.999)
    s = a0 ** (1.0 / LRU_C)
    return {
        "x_prompt": nrm(ks[0], (BATCH, SEQ, D_MODEL), 1.0),
        "x_sample": nrm(ks[1], (DEC_BATCH, DEC_SEQ, D_MODEL), 1.0),
        "cache_cmp_kv": nrm(ks[2], (n_pool, PAGE_SIZE, 2, N_KV, HEAD_DIM), 1.0),
        "cache_slc_kv": nrm(ks[3], (n_pool, PAGE_SIZE, 2, N_KV, HEAD_DIM), 1.0),
        "state_win_kv": nrm(ks[4], (DEC_BATCH, wb, 2, N_KV, HEAD_DIM), 1.0),
        "state_lru_h": nrm(ks[5], (A, DEC_BATCH, D_RNN), 0.5),
        "state_conv": nrm(ks[6], (A, DEC_BATCH, CONV_W - 1, D_RNN), 1.0),
        "page_table": jax.random.permutation(ks[7], n_pool)[:n_used].reshape(DEC_BATCH, n_pages).astype(jnp.int32),
        "a_norm": 1.0 + nrm(ks[8], (A, D_MODEL), 0.02),
        "a_w_in": nrm(ks[9], (A, D_MODEL, 2 * D_RNN), D_MODEL ** -0.5),
        "a_conv_w": nrm(ks[10], (A, CONV_W, D_RNN), CONV_W ** -0.5),
        "a_conv_b": nrm(ks[11], (A, D_RNN), 0.02),
        "a_w_rg": nrm(ks[12], (A, LRU_BLOCKS, LRU_BS, LRU_BS), LRU_BS ** -0.5),
        "a_b_rg": nrm(ks[13], (A, D_RNN), 0.02),
        "a_w_ig": nrm(ks[14], (A, LRU_BLOCKS, LRU_BS, LRU_BS), LRU_BS ** -0.5),
        "a_b_ig": nrm(ks[15], (A, D_RNN), 0.02),
        "a_lambda": jnp.log(s) - jnp.log1p(-s),
        "a_w_out": nrm(ks[17], (A, D_RNN, D_MODEL), D_RNN ** -0.5),
        "kv_norm": 1.0 + nrm(ks[18], (D_MODEL,), 0.02),
        "w_kv": nrm(ks[19], (D_MODEL, 6 * N_KV * HEAD_DIM), D_MODEL ** -0.5),
        "k_norm": 1.0 + nrm(ks[20], (3, HEAD_DIM), 0.02),
        "cmp_pos": nrm(ks[21], (2, CMP_BLK, HEAD_DIM), 0.1),
        "w_cmp1": nrm(ks[22], (2, CMP_BLK, HEAD_DIM, CMP_HID), (CMP_BLK * HEAD_DIM) ** -0.5),
        "w_cmp2": nrm(ks[23], (2, CMP_HID, HEAD_DIM), CMP_HID ** -0.5),
        "rel_table": nrm(ks[24], (N_BUCKETS, N_HEADS), 0.5),
        "b_norm": 1.0 + nrm(ks[25], (Bn, D_MODEL), 0.02),
        "b_w_in": nrm(ks[26], (Bn, D_MODEL, 2 * N_HEADS * HEAD_DIM + 3 * N_HEADS), D_MODEL ** -0.5),
        "b_gate_bias": nrm(ks[27], (Bn, 3 * N_HEADS), 0.1),
        "b_q_norm": 1.0 + nrm(ks[28], (Bn, HEAD_DIM), 0.02),
        "b_w_out": nrm(ks[29], (Bn, N_HEADS * HEAD_DIM, D_MODEL), (N_HEADS * HEAD_DIM) ** -0.5),
    }


def reference(x_prompt, x_sample, cache_cmp_kv, cache_slc_kv, state_win_kv, state_lru_h,
              state_conv, page_table, a_norm, a_w_in, a_conv_w, a_conv_b, a_w_rg, a_b_rg,
              a_w_ig, a_b_ig, a_lambda, a_w_out, kv_norm, w_kv, k_norm, cmp_pos, w_cmp1,
              w_cmp2, rel_table, b_norm, b_w_in, b_gate_bias, b_q_norm, b_w_out):
    tbl = rel_table.astype(jnp.float32).reshape(N_BUCKETS, N_KV, Q_PER_KV).transpose(1, 0, 2)
    n_pages = PAST_LEN // PAGE_SIZE
    wb = min(WINDOW, PAST_LEN)
    pos_p = jnp.arange(SEQ, dtype=jnp.int32)
    pos_s = PAST_LEN + jnp.arange(DEC_SEQ, dtype=jnp.int32)
    xp, xs = x_prompt, x_sample
    h0_p = jnp.zeros((BATCH, D_RNN), x_prompt.dtype)
    conv0_p = jnp.zeros((BATCH, CONV_W - 1, D_RNN), x_prompt.dtype)
    p_h, p_c, s_h, s_c = [], [], [], []
    for layer in range(DEPTH):
        if layer < N_A_LAYERS:
            wa = (a_norm[layer], a_w_in[layer], a_conv_w[layer], a_conv_b[layer], a_w_rg[layer],
                  a_b_rg[layer], a_w_ig[layer], a_b_ig[layer], a_lambda[layer], a_w_out[layer])
            xp, hp, cp = rglru_layer(xp, pos_p, h0_p, conv0_p, *wa)
            xs, hs, cs = rglru_layer(xs, pos_s, state_lru_h[layer], state_conv[layer], *wa)
            p_h.append(hp)
            p_c.append(cp)
            s_h.append(hs)
            s_c.append(cs)
            continue
        if layer == N_A_LAYERS:
            p_cmp_kv, p_slc_kv, p_win_rows = shared_kv_rows(xp, kv_norm, w_kv)
            s_cmp_kv, s_slc_kv, s_win_rows = shared_kv_rows(xs, kv_norm, w_kv)
            ctx_p = global_context(p_cmp_kv, p_slc_kv, k_norm, cmp_pos, w_cmp1, w_cmp2)
            past_cmp = cache_cmp_kv[page_table].reshape(DEC_BATCH, n_pages * PAGE_SIZE, 2, N_KV, HEAD_DIM)
            past_slc = cache_slc_kv[page_table].reshape(DEC_BATCH, n_pages * PAGE_SIZE, 2, N_KV, HEAD_DIM)
            ctx_s = global_context(jnp.concatenate([past_cmp, s_cmp_kv], axis=1),
                                   jnp.concatenate([past_slc, s_slc_kv], axis=1),
                                   k_norm, cmp_pos, w_cmp1, w_cmp2)
            full_win = jnp.concatenate([state_win_kv, s_win_rows], axis=1)
            win_pos = PAST_LEN - wb + jnp.arange(wb + DEC_SEQ, dtype=jnp.int32)
            attend_p = lambda q: prompt_attend(q, ctx_p, p_win_rows, k_norm, tbl)
            attend_s = lambda q: sample_attend(q, pos_s, ctx_s, full_win, win_pos, k_norm, tbl)
        li = layer - N_A_LAYERS
        wb_l = (b_norm[li], b_w_in[li], b_gate_bias[li], b_q_norm[li], b_w_out[li])
        xp = nsa_layer(xp, attend_p, *wb_l)
        xs = nsa_layer(xs, attend_s, *wb_l)
    p_win_kv = p_win_rows[:, SEQ - min(WINDOW, SEQ):]
    s_win_kv = full_win[:, full_win.shape[1] - min(WINDOW, PAST_LEN + DEC_SEQ):]
    p_lru_h = jnp.stack(p_h)
    p_conv = jnp.stack(p_c)
    s_lru_h = jnp.stack(s_h)
    s_conv = jnp.stack(s_c)
    return (xp, xs, p_cmp_kv, p_slc_kv, p_win_kv, p_lru_h, p_conv,
            s_cmp_kv, s_slc_kv, s_win_kv, s_lru_h, s_conv)
```

```python
from contextlib import ExitStack
import numpy as np
import concourse.bass as bass
import concourse.mybir as mybir
from concourse.bass_utils import run_bass_kernel_spmd

F32, BF16, I32 = mybir.dt.float32, mybir.dt.bfloat16, mybir.dt.int32
ALU = mybir.AluOpType
AF = mybir.ActivationFunctionType
EPS = 1e-6
D = 4096
T = 2048
TT = 512
NT = T // TT
NCH = 32


DEBUG_STOP = None
DEBUG_DUMP = False


class _Stop(Exception):
    pass


def ckpt(n):
    if DEBUG_STOP is not None and n == DEBUG_STOP:
        raise _Stop()


class Sched:
    ENG = ("pe", "dve", "act", "pool", "sp")

    def __init__(self, nc, es):
        self.nc = nc
        self.streams = {e: [] for e in self.ENG}
        self.sem = {e: es.enter_context(nc.semaphore("s_" + e)) for e in self.ENG}
        self.cnt = {e: 0 for e in self.ENG}
        self.seen = {e: {} for e in self.ENG}
        self.NDS = 24
        self.dsem = [es.enter_context(nc.semaphore(f"dq{i}")) for i in range(self.NDS)]
        self.duse = [0] * self.NDS
        self.qpool = {"sp": list(range(0, 16)), "pool": list(range(16, 24)), "act": list(range(0, 16))}
        self.qn = {"sp": 0, "pool": 0, "act": 0}
        self.lastw = {}
        self.readers = {}

    def _semobj(self, s):
        return self.sem[s] if isinstance(s, str) else self.dsem[s[1]]

    def _waits(self, eng, reads, writes):
        need = {}

        def add(ev):
            s, v = ev
            if s == eng and v > self.cnt[eng]:
                return
            if self.seen[eng].get(s, 0) < v:
                need[s] = max(need.get(s, 0), v)
        for k in reads:
            if k in self.lastw:
                add(self.lastw[k])
        for k in writes:
            if k in self.lastw:
                add(self.lastw[k])
            for s, v in self.readers.get(k, {}).items():
                add((s, v))
        for s, v in need.items():
            self.seen[eng][s] = v
            sem = self._semobj(s)
            self.streams[eng].append(lambda e, sem=sem, v=v: e.wait_ge(sem, v))

    def _record(self, ev, reads, writes):
        for k in writes:
            self.lastw[k] = ev
            self.readers[k] = {}
        for k in reads:
            if k not in writes:
                d = self.readers.setdefault(k, {})
                d[ev[0]] = max(d.get(ev[0], 0), ev[1])

    def op(self, eng, fn, reads=(), writes=(), inc=True):
        self._waits(eng, reads, writes)
        sem = self.sem[eng]
        if inc:
            self.cnt[eng] += 1
            self.streams[eng].append(lambda e, fn=fn, sem=sem: fn(e).then_inc(sem, 1))
            ev = (eng, self.cnt[eng])
        else:
            self.streams[eng].append(lambda e, fn=fn: fn(e))
            ev = (eng, self.cnt[eng] + 1)
        self._record(ev, reads, writes)

    def dma(self, q, out, in_, reads=(), writes=(), fn=None):
        self._waits(q, reads, writes)
        pl = self.qpool[q]
        i = pl[self.qn[q] % len(pl)]
        self.qn[q] += 1
        r = self.duse[i]
        self.duse[i] += 1
        sem = self.dsem[i]
        key = ("d", i)
        if r > 0 and self.seen[q].get(key, 0) < 16 * r:
            self.seen[q][key] = 16 * r
            self.streams[q].append(lambda e, sem=sem, v=16 * r: e.wait_ge(sem, v))
        if fn is None:
            self.streams[q].append(lambda e, sem=sem, out=out, in_=in_: e.dma_start(out=out, in_=in_).then_inc(sem, 16))
        else:
            self.streams[q].append(lambda e, sem=sem, fn=fn: fn(e).then_inc(sem, 16))
        self._record((key, 16 * (r + 1)), reads, writes)

    def flush(self):
        self.finalize()
        self.streams = {e: [] for e in self.ENG}

    def finalize(self):
        nc = self.nc
        for i in range(self.NDS):
            if self.duse[i]:
                self.streams["sp"].append(lambda e, sem=self.dsem[i], v=16 * self.duse[i]: e.wait_ge(sem, v))
        with nc.Block() as blk:
            @blk.tensor
            def _(e):
                for f in self.streams["pe"]:
                    f(e)

            @blk.vector
            def _(e):
                for f in self.streams["dve"]:
                    f(e)

            @blk.scalar
            def _(e):
                for f in self.streams["act"]:
                    f(e)

            @blk.gpsimd
            def _(e):
                for f in self.streams["pool"]:
                    f(e)

            @blk.sync
            def _(e):
                for f in self.streams["sp"]:
                    f(e)


def build(D_=4096, T_=2048):
    global D, T, NT, NCH
    D, T = D_, T_
    NT, NCH = T // TT, D // 128
    NB = D // 256
    nc = bass.Bass("TRN2", target_bir_lowering=False)
    es = ExitStack()

    def din(name, shape, dt=F32):
        return nc.dram_tensor(name, list(shape), dt, kind="ExternalInput").ap()

    def dout(name, shape, dt=F32):
        return nc.dram_tensor(name, list(shape), dt, kind="ExternalOutput").ap()

    esA = ExitStack()

    def sb(name, shape, dt=F32, stack=None):
        return (stack or es).enter_context(nc.sbuf_tensor(name, list(shape), dt))

    def sbA(name, shape, dt=F32):
        return sb(name, shape, dt, esA)

    xp = din("xp", [T, D])
    xs_in = din("xs_fm", [128, NCH])
    a_w_in = din("a_w_in", [2, D, 2 * D])
    a_w_out = din("a_w_out", [2, D, D])
    a_w_rg = din("a_w_rg", [2, NB, 256, 256])
    a_w_ig = din("a_w_ig", [2, NB, 256, 256])
    avec = din("avec", [2, 128, 9, NCH])
    kvn_in = din("kvn_fm", [128, NCH])
    st_h = din("st_h", [2, 128, NCH])
    st_conv = din("st_conv", [2, 128, NCH, 3])
    w_kv = din("w_kv", [D, 3072])
    ident_in = din("ident", [128, 128])
    st_win = din("st_win", [512, 1024])

    XR = nc.dram_tensor("xr_scratch", [T, D], F32).ap()
    NH = NCH
    R = NH // 4
    RP = min(4, R)
    NPART = R // RP
    NQT = T // 128
    NCMP = (T - 32) // 16 + 1
    NSLC = T // 64
    k_norm_h = nc.dram_tensor("k_norm", [3, 128], F32, kind="ExternalInput")
    rel_h = nc.dram_tensor("rel_table", [32, NH], F32, kind="ExternalInput")
    cmp_posT = din("cmp_posT", [128, 2, 32])
    w_cmp1 = din("w_cmp1", [2, 32, 128, 256])
    w_cmp2 = din("w_cmp2", [2, 256, 128])
    b_norm_in = din("b_norm_fm", [2, 128, NCH])
    b_w_in = din("b_w_in", [2, D, 2 * D + 3 * NH])
    b_gb = din("b_gb", [2, 3 * NH, 1])
    b_qn = din("b_qn", [2, 128, 1])
    b_w_out = din("b_w_out", [2, D, D])
    c_mimp = din("c_mimp", [128, NSLC])
    c_expm = din("c_expm", [NSLC, T])
    c_c0 = din("c_c0", [128, 128])
    c_c4 = din("c_c4", [128, 128])
    c_A = din("c_A", [128, NQT, NSLC])
    c_B = din("c_B", [128, NQT, NSLC])
    cache_cmp = din("cache_cmp", [640 * 128, 1024])
    cache_slc = din("cache_slc", [640 * 128, 1024])
    ptab_h = nc.dram_tensor("ptab", [1, 64], I32, kind="ExternalInput")
    c_mimps = din("c_mimps", [128, 4, 130])
    c_kval = din("c_kval", [128, 65])
    c_kvalc = din("c_kvalc", [128, 4])
    c_cbs = din("c_cbs", [1, 130])
    LS = 8192
    KVROW = nc.dram_tensor("kvrow_d", [1, 3072], F32).ap()
    KSs = nc.dram_tensor("kss_d", [4, 128, LS + 128], BF16).ap()
    VSs = nc.dram_tensor("vss_d", [LS + 128, 512], BF16).ap()
    KWs = nc.dram_tensor("kws_d", [4, 128, 640], BF16).ap()
    VWs = nc.dram_tensor("vws_d", [640, 512], BF16).ap()
    CMPs = nc.dram_tensor("cmps_d", [2, 4, 128, LS], BF16).ap()
    KCs = nc.dram_tensor("kcs_d", [4, 128, 512], BF16).ap()
    VCs = nc.dram_tensor("vcs_d", [512, 512], BF16).ap()
    SBM = nc.dram_tensor("sbm", [128, NH, 3], F32).ap()
    KST = nc.dram_tensor("kst", [4, 128, T], BF16).ap()
    KWT = nc.dram_tensor("kwt", [4, 128, T], BF16).ap()
    VS = nc.dram_tensor("vs", [T, 512], BF16).ap()
    VW = nc.dram_tensor("vw", [T, 512], BF16).ap()
    CMPT = nc.dram_tensor("cmpt", [2, 4, 128, T], BF16).ap()
    TBM = nc.dram_tensor("tbm", [128, NH, 256], F32).ap()
    CBM = nc.dram_tensor("cbm", [128, NH, 248], F32).ap()

    o_yp = dout("o_yp", [T, D])
    o_ys = dout("o_ys", [128, NCH])
    o_pcmp = dout("o_pcmp", [T, 1024])
    o_pslc = dout("o_pslc", [T, 1024])
    o_pwin = dout("o_pwin", [512, 1024])
    o_plh = dout("o_plh", [2, 128, NCH])
    o_pconv = dout("o_pconv", [2, 128, NCH, 3])
    o_scmp = dout("o_scmp", [1, 1024])
    o_sslc = dout("o_sslc", [1, 1024])
    o_swin = dout("o_swin", [512, 1024])
    o_slh = dout("o_slh", [2, 128, NCH])
    o_sconv = dout("o_sconv", [2, 128, NCH, 3])

    S = Sched(nc, es)

    xnT = sb("xnT", [128, NCH, TT], BF16)
    act = sb("act", [128, NCH, TT], BF16)
    ident = sb("identb", [128, 128], BF16)
    identf = sb("identf", [128, 128], F32)
    ones = sb("ones", [128, 128], F32)
    onesb = sb("onesb", [128, 128], BF16)
    kvn = sb("kvn", [128, NCH], F32)
    xs = sb("xs", [128, NCH], F32)
    xns = sb("xns", [128, NCH], BF16)
    acts = sb("acts", [128, NCH], BF16)
    ss = sb("ss", [128, 4], F32)
    xres = [sb(f"xres{i}", [128, 4, 256], F32) for i in range(2)]
    kg = sb("kg", [128, 3, 128], F32)
    kcT = sb("kcT", [128, 4, 128], BF16)
    vc = sb("vc", [128, 4, 128], BF16)
    NW = 3
    wbuf = [sbA(f"wbuf{i}", [128, NCH, 256], BF16) for i in range(NW)]
    wsm = [sbA(f"wsm{i}", [128, 2, 256], BF16) for i in range(4)]
    xst = sbA("xst", [128, D], F32)
    xn = sbA("xn", [128, D], BF16)
    avec_t = sbA("avec_t", [128, 9, NCH], F32)
    cn = sbA("cn", [128, 2, NCH], F32)
    tmpv = sbA("tmpv", [128, NCH], F32)
    hprev = sbA("hprev", [128, NCH], F32)
    xtail = sbA("xtail", [128, NCH, 3], F32)
    hs_s = sbA("hs_s", [128, NCH], F32)
    cs_s = sbA("cs_s", [128, NCH, 3], F32)
    cs_o = sbA("cs_o", [128, NCH, 3], F32)
    xbe = [sbA(f"xbe{i}", [128, TT + 3], F32) for i in range(2)]
    xc = [sbA(f"xc{i}", [128, TT], F32) for i in range(2)]
    xcb = sbA("xcb", [128, 2, TT], BF16)
    tr = sbA("tr", [128, TT], F32)
    ta = sbA("ta", [128, TT], F32)
    tm = sbA("tm", [128, TT], F32)
    ti = sbA("ti", [128, TT], F32)
    ths = sbA("ths", [128, TT], F32)
    tg = sbA("tg", [128, TT], F32)
    sxbe = [sbA(f"sxbe{i}", [128, 4], F32) for i in range(2)]
    sxc = [sbA(f"sxc{i}", [128, 1], F32) for i in range(2)]
    sxcb = sbA("sxcb", [128, 2, 1], BF16)
    stmp = sbA("stmp", [128, 8], F32)
    kvrow = sbA("kvrow", [1, 3072], F32)
    ksq = sbA("ksq", [128, 8, 128], F32)
    kbb = sbA("kbb", [128, 4, 2, 128], BF16)
    kstage = sbA("kstage", [128, 2, 4, 128], BF16)
    kss = sbA("kss", [128, 8], F32)
    cur = {"xst": xst, "xn": xn, "xk": []}

    NPS = 7
    ps = [es.enter_context(nc.psum_tensor(f"ps{i}", [128, 512], F32)) for i in range(NPS)]
    pss = es.enter_context(nc.psum_tensor("pss", [128, 512], F32))
    st = {"ps": 0, "w": 0, "xr": 0}

    def next_ps(pool=None):
        if pool is None:
            i = st["ps"] % NPS
            st["ps"] += 1
        else:
            n = st.get(("pp", pool), 0)
            st[("pp", pool)] = n + 1
            i = pool[n % len(pool)]
        return ps[i], ("ps", i)

    def next_w():
        i = st["w"] % NW
        st["w"] += 1
        return wbuf[i], ("w", i)

    S.dma("pool", ident[:], ident_in, writes=["ident"])
    S.dma("sp", identf[:], ident_in, writes=["identf"])
    S.op("dve", lambda e: e.memset(onesb[:], 1.0), writes=["onesb"])
    S.op("dve", lambda e: e.memset(ones[:], 1.0), writes=["ones"])
    S.dma("sp", xs[:], xs_in, writes=["xs"])
    S.dma("sp", kvn[:], kvn_in, writes=["kvn"])

    def load_w(src_cols):
        w, wk = next_w()
        S.dma("pool", w[:], src_cols.rearrange("(c p) n -> p c n", p=128), writes=[wk])
        return w, wk

    def make_xnT(Xsrc, tt, gvec, gkey):
        for sub in range(4):
            r0 = tt * TT + sub * 128
            xst, xn, xk = cur["xst"], cur["xn"], cur["xk"]
            S.dma("sp", xst[:], Xsrc[r0:r0 + 128, :], reads=["XR"], writes=["xst"] + xk)
            S.op("dve", lambda e: e.memset(ss[:, 0:1], 0.0), writes=["ss"])
            S.op("act", lambda e, xn=xn, xst=xst: e.activation(out=xn[:], in_=xst[:], func=AF.Square, accum_out=ss[:, 0:1]),
                 reads=["xst", "ss"], writes=["xn", "ss"] + xk)
            S.op("act", lambda e: e.activation(out=ss[:, 1:2], in_=ss[:, 0:1], func=AF.Ln, scale=1.0 / D, bias=EPS),
                 reads=["ss"], writes=["ss"])
            S.op("act", lambda e: e.activation(out=ss[:, 2:3], in_=ss[:, 1:2], func=AF.Exp, scale=-0.5),
                 reads=["ss"], writes=["ss"])
            S.op("dve", lambda e, xn=xn, xst=xst: e.tensor_scalar(out=xn[:], in0=xst[:], scalar1=ss[:, 2:3], scalar2=None,
                                                                  op0=ALU.mult), reads=["xst", "ss"], writes=["xn"] + xk)
            for g8 in range(NCH // 8):
                p, pk = next_ps()
                pv = p[:].bitcast(BF16)
                for k in range(8):
                    c = g8 * 8 + k
                    S.op("pe", lambda e, pv=pv, k=k, c=c, xn=xn: e.transpose(out=pv[:, k * 128:(k + 1) * 128],
                                                                       in_=xn[:, c * 128:(c + 1) * 128],
                                                                       identity=ident[:]),
                         reads=["xn", "ident"], writes=[pk] + xk, inc=(k == 7))
                S.op("dve", lambda e, pv=pv, g8=g8, sub=sub: e.tensor_tensor(
                    out=xnT[:, g8 * 8:(g8 + 1) * 8, sub * 128:(sub + 1) * 128],
                    in0=pv[:, 0:1024].rearrange("p (k t) -> p k t", k=8),
                    in1=gvec[:, g8 * 8:(g8 + 1) * 8].unsqueeze(2).to_broadcast([128, 8, 128]),
                    op=ALU.mult), reads=[pk, gkey], writes=["xnT"])

    def make_xns(gvec, gkey):
        S.op("dve", lambda e: e.memset(ss[:, 0:1], 0.0), writes=["ss"])
        jt = cur.get("tmpv", tmpv)
        S.op("act", lambda e, jt=jt: e.activation(out=jt[:], in_=xs[:], func=AF.Square, accum_out=ss[:, 0:1]),
             reads=["xs", "ss"], writes=["tmpv", "ss"])
        S.op("pe", lambda e: e.matmul(pss[:, 500:501], lhsT=ones[:], rhs=ss[:, 0:1], start=True, stop=True),
             reads=["ones", "ss"], writes=["pss"])
        S.op("act", lambda e: e.activation(out=ss[:, 1:2], in_=pss[:, 500:501], func=AF.Ln, scale=1.0 / D, bias=EPS),
             reads=["pss"], writes=["ss"])
        S.op("act", lambda e: e.activation(out=ss[:, 2:3], in_=ss[:, 1:2], func=AF.Exp, scale=-0.5),
             reads=["ss"], writes=["ss"])
        S.op("dve", lambda e: e.scalar_tensor_tensor(out=xns[:], in0=xs[:], scalar=ss[:, 2:3], in1=gvec,
                                                     op0=ALU.mult, op1=ALU.mult),
             reads=["xs", "ss", gkey], writes=["xns"])

    def ew1(n, psx, pkx, xbe_t, xbk, xc_t, xck, tail_ap, newtail_ap, tailkey, ntkey, j, c2, xcb_t, xcbk):
        cw = lambda k: avec_t[:, 1 + k, j:j + 1]
        cb = avec_t[:, 5, j:j + 1]
        S.op("dve", lambda e: e.tensor_copy(out=xbe_t[:, 0:3], in_=tail_ap), reads=[tailkey], writes=[xbk])
        S.op("act", lambda e: e.activation(out=xbe_t[:, 3:3 + n], in_=psx, func=AF.Copy), reads=[pkx], writes=[xbk])
        S.op("dve", lambda e: e.tensor_copy(out=newtail_ap, in_=xbe_t[:, n:n + 3]), reads=[xbk], writes=[ntkey])
        S.op("dve", lambda e: e.tensor_scalar(out=xc_t[:, 0:n], in0=xbe_t[:, 0:n], scalar1=cw(0), scalar2=cb,
                                              op0=ALU.mult, op1=ALU.add), reads=[xbk, "avec"], writes=[xck])
        for k in range(1, 4):
            S.op("dve", lambda e, k=k: e.scalar_tensor_tensor(out=xc_t[:, 0:n], in0=xbe_t[:, k:k + n], scalar=cw(k),
                                                              in1=xc_t[:, 0:n], op0=ALU.mult, op1=ALU.add),
                 reads=[xbk, xck, "avec"], writes=[xck])
        S.op("act", lambda e: e.activation(out=xcb_t[:, c2, 0:n], in_=xc_t[:, 0:n], func=AF.Copy),
             reads=[xck], writes=[xcbk])

    def ew2(n, psr, pkr, psi, pki, psg, pkg, xc_t, xck, j, tset, tkeys, h_ap, hkey, out_ap, outkey, first):
        r_, a_, m_, i_, h_, g_ = tset
        kr, ka, km, ki, kh, kg = tkeys
        brg = avec_t[:, 6, j:j + 1]
        big = avec_t[:, 7, j:j + 1]
        S.op("act", lambda e: e.activation(out=r_, in_=psr, func=AF.Sigmoid, bias=brg), reads=[pkr, "avec"], writes=[kr])
        S.op("act", lambda e: e.activation(out=i_, in_=psi, func=AF.Sigmoid, bias=big), reads=[pki, "avec"], writes=[ki])
        S.op("act", lambda e: e.activation(out=a_, in_=r_, func=AF.Exp, scale=cn[:, 0, j:j + 1]), reads=[kr, "cn"], writes=[ka])
        S.op("act", lambda e: e.activation(out=m_, in_=r_, func=AF.Exp, scale=cn[:, 1, j:j + 1]), reads=[kr, "cn"], writes=[km])
        S.op("act", lambda e: e.activation(out=g_, in_=psg, func=AF.Silu), reads=[pkg], writes=[kg])
        S.op("act", lambda e: e.activation(out=m_, in_=m_, func=AF.Sqrt, scale=-1.0, bias=1.0), reads=[km], writes=[km])
        if first:
            S.op("dve", lambda e: e.memset(m_[:, 0:1], 1.0), writes=[km])
        S.op("dve", lambda e: e.tensor_tensor(out=i_, in0=i_, in1=m_, op=ALU.mult), reads=[ki, km], writes=[ki])
        S.op("dve", lambda e: e.tensor_tensor(out=i_, in0=i_, in1=xc_t[:, 0:n], op=ALU.mult), reads=[ki, xck], writes=[ki])
        S.op("dve", lambda e: e.tensor_tensor_scan(out=h_, data0=a_, data1=i_, initial=h_ap, op0=ALU.mult, op1=ALU.add),
             reads=[ka, ki, hkey], writes=[kh])
        S.op("dve", lambda e: e.tensor_copy(out=h_ap, in_=h_[:, n - 1:n]), reads=[kh], writes=[hkey])
        S.op("dve", lambda e: e.tensor_tensor(out=out_ap, in0=h_, in1=g_, op=ALU.mult), reads=[kh, kg], writes=[outkey])

    def rglru_layer(l):
        Xin = xp if l == 0 else XR
        S.dma("sp", avec_t[:], avec[l], writes=["avec"])
        lam = avec_t[:, 8, :]
        S.op("act", lambda e: e.activation(out=tmpv[:], in_=lam, func=AF.Exp, scale=-1.0), reads=["avec"], writes=["tmpv"])
        S.op("act", lambda e: e.activation(out=tmpv[:], in_=tmpv[:], func=AF.Ln, bias=1.0), reads=["tmpv"], writes=["tmpv"])
        S.op("dve", lambda e: e.tensor_scalar(out=cn[:, 0, :], in0=tmpv[:], scalar1=-8.0, scalar2=None, op0=ALU.mult),
             reads=["tmpv"], writes=["cn"])
        S.op("dve", lambda e: e.tensor_scalar(out=cn[:, 1, :], in0=tmpv[:], scalar1=-16.0, scalar2=None, op0=ALU.mult),
             reads=["tmpv"], writes=["cn"])
        S.op("dve", lambda e: e.memset(hprev[:], 0.0), writes=["hprev"])
        S.op("dve", lambda e: e.memset(xtail[:], 0.0), writes=["xtail"])
        S.dma("sp", hs_s[:], st_h[l], writes=["hs_s"])
        S.dma("sp", cs_s[:], st_conv[l], writes=["cs_s"])
        ckpt(1)
        make_xns(avec_t[:, 0, :], "avec")
        ckpt(2)

        for tt in range(NT):
            last = tt == NT - 1
            make_xnT(Xin, tt, avec_t[:, 0, :], "avec")
            ckpt(3)
            for nb in range(NB):
                wA, wAk = load_w(a_w_in[l][:, nb * 256:(nb + 1) * 256])
                wG, wGk = load_w(a_w_in[l][:, D + nb * 256:D + (nb + 1) * 256])
                wr, wi = wsm[(nb % 2) * 2], wsm[(nb % 2) * 2 + 1]
                wrk, wik = ("wsm", (nb % 2) * 2), ("wsm", (nb % 2) * 2 + 1)
                S.dma("pool", wr[:], a_w_rg[l, nb].rearrange("(c p) n -> p c n", p=128), writes=[wrk])
                S.dma("pool", wi[:], a_w_ig[l, nb].rearrange("(c p) n -> p c n", p=128), writes=[wik])
                ckpt(35)
                pxs = []
                for (w, wk, cbase) in ((wA, wAk, 0), (wG, wGk, 4)):
                    for c2 in range(2):
                        p, pk = next_ps()
                        slot = cbase + c2
                        for dch in range(NCH):
                            S.op("pe", lambda e, p=p, w=w, c2=c2, dch=dch: e.matmul(
                                p[:], lhsT=w[:, dch, c2 * 128:(c2 + 1) * 128], rhs=xnT[:, dch, :],
                                start=(dch == 0), stop=(dch == NCH - 1)),
                                reads=[wk, "xnT"], writes=[pk], inc=(dch == NCH - 1))
                        if last:
                            for dch in range(NCH):
                                S.op("pe", lambda e, w=w, c2=c2, dch=dch, slot=slot: e.matmul(
                                    pss[:, slot:slot + 1], lhsT=w[:, dch, c2 * 128:(c2 + 1) * 128],
                                    rhs=xns[:, dch:dch + 1], start=(dch == 0), stop=(dch == NCH - 1)),
                                    reads=[wk, "xns"], writes=["pss"], inc=(dch == NCH - 1))
                        pxs.append((p, pk))
                (pxb0, kxb0), (pxb1, kxb1), (pg0, kg0), (pg1, kg1) = pxs
                ckpt(4)
                for c2, (pp, ppk) in enumerate(((pxb0, kxb0), (pxb1, kxb1))):
                    j = nb * 2 + c2
                    ew1(TT, pp[:], ppk, xbe[c2], ("xbe", c2), xc[c2], ("xc", c2), xtail[:, j, :], xtail[:, j, :],
                        "xtail", "xtail", j, c2, xcb, "xcb")
                    if last:
                        ew1(1, pss[:, c2:c2 + 1], "pss", sxbe[c2], ("sxbe", c2), sxc[c2], ("sxc", c2),
                            cs_s[:, j, :], cs_o[:, j, :], "cs_s", "cs_o", j, c2, sxcb, "sxcb")
                ckpt(5)
                for c2, (pg, pgk) in enumerate(((pg0, kg0), (pg1, kg1))):
                    j = nb * 2 + c2
                    pr, prk = next_ps()
                    pi, pik = next_ps()
                    for (pp, ppk, w, wk) in ((pr, prk, wr, wrk), (pi, pik, wi, wik)):
                        for dc in range(2):
                            S.op("pe", lambda e, pp=pp, w=w, dc=dc, c2=c2: e.matmul(
                                pp[:], lhsT=w[:, dc, c2 * 128:(c2 + 1) * 128], rhs=xcb[:, dc, :],
                                start=(dc == 0), stop=(dc == 1)), reads=[wk, "xcb"], writes=[ppk], inc=(dc == 1))
                    if last:
                        for gi, (w, wk) in enumerate(((wr, wrk), (wi, wik))):
                            for dc in range(2):
                                S.op("pe", lambda e, w=w, dc=dc, c2=c2, gi=gi: e.matmul(
                                    pss[:, 8 + gi:9 + gi], lhsT=w[:, dc, c2 * 128:(c2 + 1) * 128], rhs=sxcb[:, dc, :],
                                    start=(dc == 0), stop=(dc == 1)), reads=[wk, "sxcb"], writes=["pss"], inc=(dc == 1))
                    ew2(TT, pr[:], prk, pi[:], pik, pg[:], pgk, xc[c2], ("xc", c2), j,
                        (tr[:], ta[:], tm[:], ti[:], ths[:], tg[:]), ("tr", "ta", "tm", "ti", "ths", "tg"),
                        hprev[:, j:j + 1], "hprev", act[:, j, :], ("act", j), first=(tt == 0))
                    if last:
                        ew2(1, pss[:, 8:9], "pss", pss[:, 9:10], "pss", pss[:, 4 + c2:5 + c2], "pss", sxc[c2], ("sxc", c2), j,
                            tuple(stmp[:, q:q + 1] for q in range(6)), ("stmp",) * 6,
                            hs_s[:, j:j + 1], "hs_s", acts[:, j:j + 1], "acts", first=False)
            ckpt(6)
            rows = slice(tt * TT, (tt + 1) * TT)
            for dc in range(NB):
                w, wk = load_w(a_w_out[l][:, dc * 256:(dc + 1) * 256])
                xr_t = xres[dc % 2]
                xrk = ("xres", dc % 2)
                S.dma("sp", xr_t[:], Xin[rows, dc * 256:(dc + 1) * 256].rearrange("(s p) n -> p s n", p=128),
                      reads=["XR"], writes=[xrk])
                for sub in range(4):
                    p, pk = next_ps()
                    for ch in range(NCH):
                        S.op("pe", lambda e, p=p, w=w, ch=ch, sub=sub: e.matmul(
                            p[:, 0:256], lhsT=act[:, ch, sub * 128:(sub + 1) * 128], rhs=w[:, ch, :],
                            start=(ch == 0), stop=(ch == NCH - 1)),
                            reads=[wk, ("act", ch)], writes=[pk], inc=(ch == NCH - 1))
                    S.op("dve", lambda e, p=p, xr_t=xr_t, sub=sub: e.tensor_tensor(
                        out=xr_t[:, sub, :], in0=p[:, 0:256], in1=xr_t[:, sub, :], op=ALU.add),
                        reads=[pk, xrk], writes=[xrk])
                S.dma("sp", XR[rows, dc * 256:(dc + 1) * 256].rearrange("(s p) n -> p s n", p=128), xr_t[:],
                      reads=[xrk], writes=["XR"])
                if last:
                    for q4 in range(2):
                        col = 16 + dc * 2 + q4
                        for ch in range(NCH):
                            S.op("pe", lambda e, w=w, ch=ch, q4=q4, col=col: e.matmul(
                                pss[:, col:col + 1], lhsT=w[:, ch, q4 * 128:(q4 + 1) * 128], rhs=acts[:, ch:ch + 1],
                                start=(ch == 0), stop=(ch == NCH - 1)),
                                reads=[wk, "acts"], writes=["pss"], inc=(ch == NCH - 1))
            if last:
                S.op("dve", lambda e: e.tensor_tensor(out=xs[:], in0=xs[:], in1=pss[:, 16:16 + NCH], op=ALU.add),
                     reads=["xs", "pss"], writes=["xs"])
        S.dma("sp", o_plh[l], hprev[:], reads=["hprev"])
        S.dma("sp", o_pconv[l], xtail[:], reads=["xtail"])
        S.dma("sp", o_slh[l], hs_s[:], reads=["hs_s"])
        S.dma("sp", o_sconv[l], cs_o[:], reads=["cs_o"])

    def kv_proj():
        S.dma("sp", kg[:].rearrange("p a d -> p (a d)"), bass.AP(k_norm_h, 0, [[0, 128], [1, 384]]), writes=["kg"])
        make_xns(kvn[:], "kvn")
        for tt in range(NT):
            last = tt == NT - 1
            make_xnT(XR, tt, kvn[:], "kvn")
            rows = slice(tt * TT, (tt + 1) * TT)
            for cc in range(12):
                w, wk = load_w(w_kv[:, cc * 256:(cc + 1) * 256])
                kv_t = xres[cc % 2]
                kvk = ("xres", cc % 2)
                for sub in range(4):
                    p, pk = next_ps()
                    for dch in range(NCH):
                        S.op("pe", lambda e, p=p, w=w, dch=dch, sub=sub: e.matmul(
                            p[:, 0:256], lhsT=xnT[:, dch, sub * 128:(sub + 1) * 128], rhs=w[:, dch, :],
                            start=(dch == 0), stop=(dch == NCH - 1)),
                            reads=[wk, "xnT"], writes=[pk], inc=(dch == NCH - 1))
                    S.op("act", lambda e, p=p, kv_t=kv_t, sub=sub: e.activation(out=kv_t[:, sub, :], in_=p[:, 0:256], func=AF.Copy),
                         reads=[pk], writes=[kvk])
                br, c0 = cc // 4, (cc % 4) * 256
                if br == 0:
                    dst = o_pcmp[rows, c0:c0 + 256]
                elif br == 1:
                    dst = o_pslc[rows, c0:c0 + 256]
                else:
                    dst = o_pwin[:, c0:c0 + 256] if last else None
                if dst is not None:
                    S.dma("sp", dst.rearrange("(s p) n -> p s n", p=128), kv_t[:], reads=[kvk])
                kind = cc % 4
                g0 = 2 * (kind % 2)
                if kind >= 2 and br > 0:
                    S.dma("pool", (VS if br == 1 else VW)[rows, g0 * 128:(g0 + 2) * 128].rearrange("(s p) n -> p s n", p=128),
                          kv_t[:], reads=[kvk])
                else:
                    kv4 = kv_t[:].rearrange("p s (g d) -> p s g d", g=2)
                    if br == 0:
                        S.op("dve", lambda e, kv4=kv4: e.tensor_copy(out=kbb[:], in_=kv4), reads=[kvk], writes=["kbb"])
                        dstT = CMPT[0 if kind < 2 else 1, g0:g0 + 2]
                    else:
                        kv8 = kv_t[:].rearrange("p s (g d) -> p (s g) d", g=2)
                        S.op("act", lambda e, kv8=kv8: e.activation(out=ksq[:], in_=kv8, func=AF.Square), reads=[kvk], writes=["ksq"])
                        S.op("dve", lambda e: e.tensor_reduce(out=kss[:], in_=ksq[:], axis=mybir.AxisListType.X, op=ALU.add),
                             reads=["ksq"], writes=["kss"])
                        S.op("act", lambda e: e.activation(out=kss[:], in_=kss[:], func=AF.Ln, scale=1.0 / 128, bias=EPS),
                             reads=["kss"], writes=["kss"])
                        S.op("act", lambda e: e.activation(out=kss[:], in_=kss[:], func=AF.Exp, scale=-0.5),
                             reads=["kss"], writes=["kss"])
                        S.op("dve", lambda e, kv8=kv8: e.tensor_tensor(out=ksq[:], in0=kv8, in1=kss[:].unsqueeze(2).to_broadcast([128, 8, 128]),
                                                                        op=ALU.mult), reads=[kvk, "kss", "ksq"], writes=["ksq"])
                        S.op("dve", lambda e, br=br: e.tensor_tensor(out=kbb[:].rearrange("p s g d -> p (s g) d"), in0=ksq[:],
                                                                      in1=kg[:, br, :].unsqueeze(1).to_broadcast([128, 8, 128]), op=ALU.mult),
                             reads=["ksq", "kg"], writes=["kbb"])
                        dstT = (KST if br == 1 else KWT)[g0:g0 + 2]
                    p, pk = next_ps()
                    pv = p[:].bitcast(BF16)
                    for gl in range(2):
                        for sub in range(4):
                            o = (gl * 4 + sub) * 128
                            S.op("pe", lambda e, pv=pv, o=o, gl=gl, sub=sub: e.transpose(out=pv[:, o:o + 128], in_=kbb[:, sub, gl, :], identity=ident[:]),
                                 reads=["kbb", "ident"], writes=[pk], inc=(gl == 1 and sub == 3))
                    S.op("act", lambda e, pv=pv: e.activation(out=kstage[:].rearrange("p g s t -> p (g s t)"), in_=pv[:, 0:1024], func=AF.Copy),
                         reads=[pk], writes=["kstage"])
                    S.dma("sp", dstT[:, :, tt * TT:(tt + 1) * TT].rearrange("g h t -> h g t"),
                          kstage[:].rearrange("p g s t -> p g (s t)"), reads=["kstage"], writes=["KVT"])
                if last:
                    for dch in range(NCH):
                        S.op("pe", lambda e, w=w, dch=dch: e.matmul(
                            pss[0:1, 0:256], lhsT=xns[:, dch:dch + 1], rhs=w[:, dch, :],
                            start=(dch == 0), stop=(dch == NCH - 1)),
                            reads=[wk, "xns"], writes=["pss"], inc=(dch == NCH - 1))
                    S.op("act", lambda e, cc=cc: e.activation(out=kvrow[0:1, cc * 256:(cc + 1) * 256], in_=pss[0:1, 0:256], func=AF.Copy),
                         reads=["pss"], writes=["kvrow"])
        S.dma("sp", KVROW, kvrow[0:1, :], reads=["kvrow"], writes=["KVROW"])
        S.dma("sp", o_scmp, kvrow[0:1, 0:1024], reads=["kvrow"])
        S.dma("sp", o_sslc, kvrow[0:1, 1024:2048], reads=["kvrow"])
        S.dma("sp", o_swin[511:512, :], kvrow[0:1, 2048:3072], reads=["kvrow"])
        S.dma("sp", o_swin[0:511, :], st_win[1:512, :])

    def _bucket_thr():
        import math
        d = np.arange(0, 400)
        df = np.maximum(d, 1).astype(np.float32)
        large = 16 + (np.log(df / np.float32(16)) / np.float32(math.log(128 / 16)) * np.float32(16)).astype(np.int32)
        b = np.where(d < 16, d, np.minimum(large, 31))
        return [int(np.argmax(b >= k)) for k in range(32)]

    def bias_phase():
        esP = ExitStack()
        thr = _bucket_thr()
        relB = sb("relB", [128, 32, NH], F32, esP)
        coef = sb("coef", [128, 32, NH], F32, esP)
        Di = sb("Di", [128, 256], I32, esP)
        Df = sb("Df", [128, 256], F32, esP)
        ind = sb("ind", [128, 256], F32, esP)
        acc = sb("bacc", [128, NH, 256], F32, esP)
        tmp = sb("btmp", [128, NH, 256], F32, esP)
        S.dma("sp", relB[:].rearrange("p b h -> p (b h)"), bass.AP(rel_h, 0, [[0, 128], [1, 32 * NH]]), writes=["relB"])
        S.op("dve", lambda e: e.tensor_tensor(out=coef[:, 1:32, :], in0=relB[:, 1:32, :], in1=relB[:, 0:31, :], op=ALU.subtract),
             reads=["relB"], writes=["coef"])
        S.op("dve", lambda e: e.tensor_tensor(out=coef[:, 0, :], in0=relB[:, 0, :], in1=relB[:, 31, :], op=ALU.subtract),
             reads=["relB"], writes=["coef"])
        for (dst, ncol, pat, base, cm, neg) in ((TBM, 256, [[1, 256]], 0, -1, False), (CBM, 248, [[-16, 248]], 1889, 1, True)):
            S.op("pool", lambda e, pat=pat, base=base, cm=cm, ncol=ncol: e.iota(Di[:, 0:ncol], pat, base=base, channel_multiplier=cm),
                 writes=["Di"])
            S.op("dve", lambda e, ncol=ncol: e.tensor_copy(out=Df[:, 0:ncol], in_=Di[:, 0:ncol]), reads=["Di"], writes=["Df"])
            S.op("dve", lambda e, ncol=ncol: e.tensor_copy(out=acc[:, :, 0:ncol], in_=coef[:, 0, :].unsqueeze(2).to_broadcast([128, NH, ncol])),
                 reads=["coef"], writes=["bacc"])
            for b in range(1, 32):
                S.op("dve", lambda e, b=b, ncol=ncol: e.tensor_scalar(out=ind[:, 0:ncol], in0=Df[:, 0:ncol], scalar1=float(thr[b]) - 0.5,
                                                                      scalar2=None, op0=ALU.is_ge), reads=["Df"], writes=["ind"])
                S.op("dve", lambda e, b=b, ncol=ncol: e.tensor_tensor(
                    out=tmp[:, :, 0:ncol], in0=coef[:, b, :].unsqueeze(2).to_broadcast([128, NH, ncol]),
                    in1=ind[:, 0:ncol].unsqueeze(1).to_broadcast([128, NH, ncol]), op=ALU.mult),
                    reads=["coef", "ind"], writes=["btmp"])
                S.op("dve", lambda e, ncol=ncol: e.tensor_tensor(out=acc[:, :, 0:ncol], in0=acc[:, :, 0:ncol], in1=tmp[:, :, 0:ncol], op=ALU.add),
                     reads=["bacc", "btmp"], writes=["bacc"])
            if neg:
                S.op("dve", lambda e, ncol=ncol: e.tensor_scalar(out=ind[:, 0:ncol], in0=Df[:, 0:ncol], scalar1=-0.5, scalar2=-1e30,
                                                                 op0=ALU.is_lt, op1=ALU.mult), reads=["Df"], writes=["ind"])
                S.op("dve", lambda e, ncol=ncol: e.tensor_tensor(out=acc[:, :, 0:ncol], in0=acc[:, :, 0:ncol],
                                                                 in1=ind[:, 0:ncol].unsqueeze(1).to_broadcast([128, NH, ncol]), op=ALU.add),
                     reads=["bacc", "ind"], writes=["bacc"])
            S.dma("sp", dst, acc[:, :, 0:ncol], reads=["bacc"], writes=["BIAS"])
        S.op("pool", lambda e: e.iota(Di[:, 0:2], [[-128, 2]], base=128, channel_multiplier=-1), writes=["Di"])
        S.op("pool", lambda e: e.iota(Di[:, 2:3], [[0, 1]], base=2017, channel_multiplier=-16), writes=["Di"])
        S.op("dve", lambda e: e.tensor_copy(out=Df[:, 0:3], in_=Di[:, 0:3]), reads=["Di"], writes=["Df"])
        S.op("dve", lambda e: e.tensor_copy(out=acc[:, :, 0:3], in_=coef[:, 0, :].unsqueeze(2).to_broadcast([128, NH, 3])),
             reads=["coef"], writes=["bacc"])
        for b in range(1, 32):
            S.op("dve", lambda e, b=b: e.tensor_scalar(out=ind[:, 0:3], in0=Df[:, 0:3], scalar1=float(thr[b]) - 0.5,
                                                       scalar2=None, op0=ALU.is_ge), reads=["Df"], writes=["ind"])
            S.op("dve", lambda e, b=b: e.tensor_tensor(
                out=tmp[:, :, 0:3], in0=coef[:, b, :].unsqueeze(2).to_broadcast([128, NH, 3]),
                in1=ind[:, 0:3].unsqueeze(1).to_broadcast([128, NH, 3]), op=ALU.mult), reads=["coef", "ind"], writes=["btmp"])
            S.op("dve", lambda e: e.tensor_tensor(out=acc[:, :, 0:3], in0=acc[:, :, 0:3], in1=tmp[:, :, 0:3], op=ALU.add),
                 reads=["bacc", "btmp"], writes=["bacc"])
        S.dma("sp", SBM, acc[:, :, 0:3], reads=["bacc"], writes=["BIAS"])
        S.flush()
        esP.close()

    def compress_phase(sample):
        esC = ExitStack()
        L = LS if sample else T
        ncmp = (L + (1 if sample else 0) - 32) // 16 + 1
        SRC = CMPs if sample else CMPT
        pre = "s" if sample else "p"
        w1 = sb(pre + "w1", [128, 2, 32, 256], BF16, esC)
        w2 = sb(pre + "w2", [128, 2, 2, 128], BF16, esC)
        posT = sb(pre + "posT", [128, 2, 32], BF16, esC)
        cpos = sb(pre + "cpos", [128, 4], F32, esC)
        ct = [sb(f"{pre}ct{i}", [128, L], BF16, esC) for i in range(2)]
        hid = sb(pre + "hid", [128, 2, 128], BF16, esC)
        kcn = sb(pre + "kcn", [128, 128], BF16, esC)
        junk = sb(pre + "cjunk", [128, 128], F32, esC)
        cst = sb(pre + "cst", [128, 128], BF16, esC)
        for s_ in range(2):
            S.dma("pool", w1[:, s_], w_cmp1[s_].rearrange("l d h -> d l h"), writes=["w1"])
        S.dma("pool", w2[:], w_cmp2.rearrange("s (c p) d -> p s c d", p=128), writes=["w2"])
        S.dma("pool", posT[:], cmp_posT, writes=["posT"])
        for s_ in range(2):
            for hc in range(2):
                col = s_ * 2 + hc
                for l in range(32):
                    S.op("pe", lambda e, s_=s_, hc=hc, l=l, col=col: e.matmul(
                        pss[:, 100 + col:101 + col], lhsT=w1[:, s_, l, hc * 128:(hc + 1) * 128], rhs=posT[:, s_, l:l + 1],
                        start=(l == 0), stop=(l == 31)), reads=["w1", "posT"], writes=["pss"], inc=(l == 31))
        S.op("dve", lambda e: e.tensor_copy(out=cpos[:], in_=pss[:, 100:104]), reads=["pss"], writes=["cpos"])
        if sample:
            S.op("dve", lambda e: e.memset(cst[:], 0.0), writes=["cst"])
            for g in range(4):
                S.dma("sp", KCs[g][:, 384:512], cst[:], reads=["cst"], writes=["KCV"])
                S.dma("sp", VCs[384:512, g * 128:(g + 1) * 128], cst[:], reads=["cst"], writes=["KCV"])
        n = 0
        for s_ in range(2):
            for g in range(4):
                c_t = ct[n % 2]
                ck = ("ct", n % 2)
                n += 1
                S.dma("sp", c_t[:], SRC[s_, g], reads=["KVT"], writes=[ck])
                for nt in range((ncmp + 127) // 128):
                    nn = min(128, ncmp - nt * 128)
                    c0_ = nt * 2048
                    for hc in range(2):
                        p, pk = next_ps()
                        for l in range(32):
                            S.op("pe", lambda e, p=p, s_=s_, hc=hc, l=l, c_t=c_t, nn=nn, c0_=c0_: e.matmul(
                                p[:, 0:nn], lhsT=w1[:, s_, l, hc * 128:(hc + 1) * 128],
                                rhs=c_t[:, c0_ + l:c0_ + l + 16 * (nn - 1) + 1:16], start=(l == 0), stop=(l == 31)),
                                reads=["w1", ck], writes=[pk], inc=(l == 31))
                        S.op("act", lambda e, p=p, hc=hc, s_=s_, nn=nn: e.activation(out=hid[:, hc, 0:nn], in_=p[:, 0:nn], func=AF.Silu,
                                                                                     bias=cpos[:, s_ * 2 + hc:s_ * 2 + hc + 1]),
                             reads=[pk, "cpos"], writes=["hid"])
                    p, pk = next_ps()
                    for hc in range(2):
                        S.op("pe", lambda e, p=p, hc=hc, s_=s_, nn=nn: e.matmul(p[0:nn, 0:128], lhsT=hid[:, hc, 0:nn], rhs=w2[:, s_, hc, :],
                                                                                 start=(hc == 0), stop=(hc == 1)),
                             reads=["hid", "w2"], writes=[pk], inc=(hc == 1))
                    if s_ == 1:
                        if sample:
                            S.op("act", lambda e, p=p, nn=nn: e.activation(out=cst[0:nn, :], in_=p[0:nn, 0:128], func=AF.Copy),
                                 reads=[pk], writes=["cst"])
                            S.dma("sp", VCs[nt * 128:nt * 128 + nn, g * 128:(g + 1) * 128], cst[0:nn, :], reads=["cst"], writes=["KCV"])
                        else:
                            S.op("act", lambda e, p=p, g=g, nn=nn: e.activation(out=vc[0:nn, g, :], in_=p[0:nn, 0:128], func=AF.Copy),
                                 reads=[pk], writes=["vc"])
                    else:
                        S.op("dve", lambda e: e.memset(ss[:, 0:1], 0.0), writes=["ss"])
                        S.op("act", lambda e, p=p, nn=nn: e.activation(out=junk[0:nn, :], in_=p[0:nn, 0:128], func=AF.Square,
                                                                       accum_out=ss[0:nn, 0:1]), reads=[pk, "ss"], writes=["cjunk", "ss"])
                        S.op("act", lambda e, nn=nn: e.activation(out=ss[0:nn, 1:2], in_=ss[0:nn, 0:1], func=AF.Ln, scale=1.0 / 128, bias=EPS),
                             reads=["ss"], writes=["ss"])
                        S.op("act", lambda e, nn=nn: e.activation(out=ss[0:nn, 2:3], in_=ss[0:nn, 1:2], func=AF.Exp, scale=-0.5),
                             reads=["ss"], writes=["ss"])
                        S.op("dve", lambda e, p=p, nn=nn: e.scalar_tensor_tensor(out=kcn[0:nn, :], in0=p[0:nn, 0:128], scalar=ss[0:nn, 2:3],
                                                                                 in1=kg[0:nn, 0, :], op0=ALU.mult, op1=ALU.mult),
                             reads=[pk, "ss", "kg"], writes=["kcn"])
                        p2, pk2 = next_ps()
                        pv = p2[:].bitcast(BF16)
                        S.op("pe", lambda e, pv=pv, nn=nn: e.transpose(out=pv[:, 0:nn], in_=kcn[0:nn, :], identity=ident[0:nn, 0:nn]),
                             reads=["kcn", "ident"], writes=[pk2])
                        if sample:
                            S.op("act", lambda e, pv=pv, nn=nn: e.activation(out=cst[:, 0:nn], in_=pv[:, 0:nn], func=AF.Copy),
                                 reads=[pk2], writes=["cst"])
                            S.dma("sp", KCs[g][:, nt * 128:nt * 128 + nn], cst[:, 0:nn], reads=["cst"], writes=["KCV"])
                        else:
                            S.op("act", lambda e, pv=pv, g=g, nn=nn: e.activation(out=kcT[:, g, 0:nn], in_=pv[:, 0:nn], func=AF.Copy),
                                 reads=[pk2], writes=["kcT"])
        S.flush()
        esC.close()

    def sample_prep():
        esS = ExitStack()
        sS = lambda name, shape, dt=F32: sb(name, shape, dt, esS)
        ptb = sS("ptb", [128, 64], I32)
        idx = sS("idx", [128, 64], I32)
        ipp = sS("ipp", [128, 1], I32)
        pg = [sS(f"pg{i}", [128, 1024], F32) for i in range(2)]
        sq = sS("ssq", [128, 4, 128], F32)
        sss = sS("sss", [128, 4], F32)
        kb8 = sS("kb8", [128, 8, 128], BF16)
        kst8 = sS("kst8", [128, 8, 128], BF16)
        S.dma("sp", ptb[:], bass.AP(ptab_h, 0, [[0, 128], [1, 64]]), writes=["ptb"])
        S.op("pool", lambda e: e.iota(ipp[:], [[0, 1]], base=0, channel_multiplier=1), writes=["ipp"])
        pf = sS("pf", [128, 64], F32)
        ipf = sS("ipf", [128, 1], F32)
        S.op("dve", lambda e: e.tensor_copy(out=pf[:], in_=ptb[:]), reads=["ptb"], writes=["pf"])
        S.op("dve", lambda e: e.tensor_copy(out=ipf[:], in_=ipp[:]), reads=["ipp"], writes=["ipf"])
        S.op("dve", lambda e: e.tensor_scalar(out=pf[:], in0=pf[:], scalar1=128.0, scalar2=ipf[:, 0:1], op0=ALU.mult, op1=ALU.add),
             reads=["pf", "ipf"], writes=["pf"])
        S.op("dve", lambda e: e.tensor_copy(out=idx[:], in_=pf[:]), reads=["pf"], writes=["idx"])
        cnt = {"n": 0}

        def tile_kv(load, knorm, dstK, dstV, cols, raw8=False):
            t_ = pg[cnt["n"] % 2]
            tk = ("pg", cnt["n"] % 2)
            cnt["n"] += 1
            load(t_, tk)
            if raw8:
                S.op("dve", lambda e, t_=t_: e.tensor_copy(out=kb8[:], in_=t_[:].rearrange("p (a d) -> p a d", a=8)), reads=[tk], writes=["kb8"])
                na = 8
            else:
                k4 = t_[:, 0:512].rearrange("p (g d) -> p g d", g=4)
                S.op("act", lambda e, k4=k4: e.activation(out=sq[:], in_=k4, func=AF.Square), reads=[tk], writes=["ssq"])
                S.op("dve", lambda e: e.tensor_reduce(out=sss[:], in_=sq[:], axis=mybir.AxisListType.X, op=ALU.add), reads=["ssq"], writes=["sss"])
                S.op("act", lambda e: e.activation(out=sss[:], in_=sss[:], func=AF.Ln, scale=1.0 / 128, bias=EPS), reads=["sss"], writes=["sss"])
                S.op("act", lambda e: e.activation(out=sss[:], in_=sss[:], func=AF.Exp, scale=-0.5), reads=["sss"], writes=["sss"])
                S.op("dve", lambda e, k4=k4: e.tensor_tensor(out=sq[:], in0=k4, in1=sss[:].unsqueeze(2).to_broadcast([128, 4, 128]), op=ALU.mult),
                     reads=[tk, "sss", "ssq"], writes=["ssq"])
                S.op("dve", lambda e, knorm=knorm: e.tensor_tensor(out=kb8[:, 0:4, :], in0=sq[:], in1=kg[:, knorm, :].unsqueeze(1).to_broadcast([128, 4, 128]),
                                                                    op=ALU.mult), reads=["ssq", "kg"], writes=["kb8"])
                na = 4
                S.dma("pool", dstV, t_[:, 512:1024], reads=[tk], writes=["SKV"])
            p, pk = next_ps()
            pv = p[:].bitcast(BF16)
            for a_ in range(na):
                S.op("pe", lambda e, pv=pv, a_=a_: e.transpose(out=pv[:, a_ * 128:(a_ + 1) * 128], in_=kb8[:, a_, :], identity=ident[:]),
                     reads=["kb8", "ident"], writes=[pk], inc=(a_ == na - 1))
            S.op("act", lambda e, pv=pv, na=na: e.activation(out=kst8[:, 0:na, :].rearrange("p a t -> p (a t)"), in_=pv[:, 0:na * 128], func=AF.Copy),
                 reads=[pk], writes=["kst8"])
            S.dma("sp", dstK, kst8[:, 0:na, :], reads=["kst8"], writes=["SKV"])

        def gather(table, j):
            def load(t_, tk):
                S.dma("pool", None, None, reads=["idx"], writes=[tk],
                      fn=lambda e, t_=t_: e.indirect_dma_start(out=t_[:], out_offset=None, in_=table,
                                                                in_offset=bass.IndirectOffsetOnAxis(ap=idx[:, j:j + 1], axis=0)))
            return load

        def newrow(c0_):
            def load(t_, tk):
                S.op("dve", lambda e, t_=t_: e.memset(t_[:], 0.0), writes=[tk])
                S.dma("sp", t_[0:1, :], KVROW[0:1, c0_:c0_ + 1024], reads=["KVROW"], writes=[tk])
            return load

        def static_rows(src):
            def load(t_, tk):
                S.dma("sp", t_[:], src, writes=[tk])
            return load

        for j in range(64):
            cs_ = slice(j * 128, (j + 1) * 128)
            tile_kv(gather(cache_slc, j), 1, KSs[:, :, cs_].rearrange("g h t -> h g t"), VSs[cs_, :], cs_)
            tile_kv(gather(cache_cmp, j), None, CMPs.rearrange("s g h t -> (s g) h t")[:, :, cs_].rearrange("a h t -> h a t"), None, cs_, raw8=True)
        tile_kv(newrow(1024), 1, KSs[:, :, LS:LS + 128].rearrange("g h t -> h g t"), VSs[LS:LS + 128, :], None)
        for wt in range(4):
            cs_ = slice(wt * 128, (wt + 1) * 128)
            tile_kv(static_rows(st_win[cs_, :]), 2, KWs[:, :, cs_].rearrange("g h t -> h g t"), VWs[cs_, :], cs_)
        tile_kv(newrow(2048), 2, KWs[:, :, 512:640].rearrange("g h t -> h g t"), VWs[512:640, :], None)
        S.flush()
        esS.close()

    def nsa_phase():
        esD = ExitStack()
        sD = lambda name, shape, dt=F32: sb(name, shape, dt, esD)
        NWD = 3
        wD = [sD(f"wD{i}", [128, NCH, 128], BF16) for i in range(NWD)]
        wbg = sD("wbg", [128, NCH, 3 * NH], BF16)
        bnv = sD("bnv", [128, NCH], F32)
        qgs = sD("qgs", [128, 1], F32)
        gbv = sD("gbv", [128, 1], F32)
        bgT = sD("bgT", [128, TT], F32)
        qT = sD("qT", [128, R, TT], BF16)
        sg = sD("sg", [128, R, TT], BF16)
        qsq = sD("qsq", [128, TT], BF16)
        rst = sD("rst", [128, TT], F32)
        kTs = sD("kTs", [128, T], BF16)
        vS = sD("vS", [128, NQT, 128], BF16)
        kTw = sD("kTw", [128, 1024], BF16)
        vW = sD("vW", [128, 8, 128], BF16)
        TBg = sD("TBg", [128, R, 256], F32)
        CBg = sD("CBg", [128, R, 248], F32)
        mimp = sD("mimp", [128, NSLC], BF16)
        expm = sD("expm", [NSLC, T], BF16)
        c0 = sD("c0", [128, 128], BF16)
        c4 = sD("c4", [128, 128], BF16)
        cA = sD("cA", [128, NQT, NSLC], F32)
        cB = sD("cB", [128, NQT, NSLC], F32)
        Lc = sD("Lc", [128, R, 128], F32)
        pc = sD("pc", [128, R, 128], BF16)
        pcT = sD("pcT", [128, R, 128], BF16)
        rs = sD("rs", [128, 2, R], F32)
        ocmp = sD("ocmp", [128, R, 128], F32)
        sc = sD("sc", [128, NSLC], F32)
        sc2 = sD("sc2", [128, NSLC], F32)
        mx1 = sD("mx1", [128, 8], F32)
        mx2 = sD("mx2", [128, 8], F32)
        selb = sD("selb", [128, NSLC], BF16)
        selT = sD("selT", [NSLC, 128], BF16)
        mk = sD("mk", [128, NQT, 128], BF16)
        Lt = sD("Lt", [128, RP, 128], F32)
        Et = [sD(f"Et{i}", [128, RP, 128], BF16) for i in range(2)]
        pTt = [sD(f"pTt{i}", [128, RP, 128], BF16) for i in range(2)]
        rden = sD("rden", [128, RP, 128], F32)
        oacc = sD("oacc", [128, RP, 128], F32)
        otmp = sD("otmp", [128, RP, 128], F32)
        qcs = sD("qcs", [128, NH], F32)
        qns = sD("qns", [128, NH], BF16)
        sgs = sD("sgs", [128, NH], F32)
        bgs = sD("bgs", [128, 1], F32)
        bgBs = sD("bgBs", [128, 3 * NH], F32)
        MKs = sD("MKs", [128, 65], BF16)
        kval = sD("kval", [128, 65], F32)
        kvalc = sD("kvalc", [128, 4], F32)
        cbs = sD("cbs", [1, 130], F32)
        mimps = sD("mimps", [128, 4, 130], BF16)
        sbm = sD("sbm_t", [128, NH, 3], F32)
        kcs = sD("kcs", [128, 512], BF16)
        vcs = sD("vcs", [128, 4, 128], BF16)
        sE = sD("sE", [128, 16, R], F32)
        sEb = sD("sEb", [128, 16, R], BF16)
        sP = sD("sP", [128, 16, R], BF16)
        sO = sD("sO", [128, 3, R], F32)
        sOg = sD("sOg", [128, 3, R], F32)
        s1 = sD("s1", [128, R], F32)
        s2 = sD("s2", [128, R], F32)
        s3 = sD("s3", [128, 4], F32)
        s3b = sD("s3b", [128, 4], BF16)
        srow = sD("srow", [1, 130], F32)
        srow2 = sD("srow2", [1, 130], F32)
        smx = sD("smx", [1, 16], F32)
        tmpvD = sD("tmpvD", [128, NCH], F32)
        cur["tmpv"] = tmpvD
        N = RP * 128
        actf = act[:].rearrange("p c t -> p (c t)").bitcast(F32)
        class _V:
            def __init__(self, ap):
                self.ap = ap
            def __getitem__(self, k):
                return self.ap[k]
        assert NCH * TT // 2 >= D + D // 2
        cur["xst"] = _V(actf[:, 0:D])
        cur["xn"] = _V(actf[:, D:D + D // 2].bitcast(BF16))
        cur["xk"] = ["actall"]
        wst = {"n": 0}

        def load_wD(src_cols):
            i = wst["n"] % NWD
            wst["n"] += 1
            S.dma("pool", wD[i][:], src_cols.rearrange("(c p) n -> p c n", p=128), writes=[("wD", i)])
            return wD[i], ("wD", i)

        S.dma("pool", mimp[:], c_mimp, writes=["mimp"])
        S.dma("pool", expm[:], c_expm, writes=["expm"])
        S.dma("pool", c0[:], c_c0, writes=["c0"])
        S.dma("pool", c4[:], c_c4, writes=["c4"])
        S.dma("sp", cA[:], c_A, writes=["cA"])
        S.dma("sp", cB[:], c_B, writes=["cB"])
        S.dma("sp", kval[:], c_kval, writes=["kval"])
        S.dma("sp", kvalc[:], c_kvalc, writes=["kvalc"])
        S.dma("sp", cbs[:], c_cbs, writes=["cbs"])
        S.dma("pool", mimps[:], c_mimps, writes=["mimps"])
        S.dma("sp", sbm[:], SBM, reads=["BIAS"], writes=["sbm"])
        PA, PB = (0, 1), (2, 3)
        PM = (4, 5, 6)

        def attention(g, i, qt):
            qc = slice(qt * 128, (qt + 1) * 128)
            off = 120 - 8 * i
            for part in range(NPART):
                p, pk = next_ps(PM)
                for rr in range(RP):
                    r = part * RP + rr
                    S.op("pe", lambda e, p=p, rr=rr, r=r: e.matmul(p[:, rr * 128:rr * 128 + NCMP], lhsT=qT[:, r, qc], rhs=kcT[:, g, 0:NCMP],
                                                                   start=True, stop=True), reads=["qT", "kcT"], writes=[pk], inc=(rr == RP - 1))
                S.op("dve", lambda e, p=p, part=part: e.tensor_tensor(
                    out=Lc[:, part * RP:(part + 1) * RP, 0:NCMP], in0=p[:, 0:RP * 128].rearrange("p (r n) -> p r n", r=RP)[:, :, 0:NCMP],
                    in1=CBg[:, part * RP:(part + 1) * RP, off:off + NCMP], op=ALU.add), reads=[pk, "CBg"], writes=["Lc"])
            S.op("act", lambda e: e.activation(out=Lc[:, :, 0:NCMP], in_=Lc[:, :, 0:NCMP], func=AF.Exp), reads=["Lc"], writes=["Lc"])
            S.op("dve", lambda e: e.tensor_reduce(out=rs[:, 0, :], in_=Lc[:, :, 0:NCMP], axis=mybir.AxisListType.X, op=ALU.add),
                 reads=["Lc"], writes=["rs"])
            S.op("dve", lambda e: e.tensor_scalar(out=rs[:, 0, :], in0=rs[:, 0, :], scalar1=1e-30, scalar2=None, op0=ALU.add),
                 reads=["rs"], writes=["rs"])
            S.op("dve", lambda e: e.reciprocal(out=rs[:, 1, :], in_=rs[:, 0, :]), reads=["rs"], writes=["rs"])
            S.op("dve", lambda e: e.tensor_tensor(out=pc[:, :, 0:NCMP], in0=Lc[:, :, 0:NCMP],
                                                  in1=rs[:, 1, :].unsqueeze(2).to_broadcast([128, R, NCMP]), op=ALU.mult),
                 reads=["Lc", "rs"], writes=["pc"])
            for part in range(NPART):
                p, pk = next_ps(PM)
                pv = p[:].bitcast(BF16)
                for rr in range(RP):
                    r = part * RP + rr
                    S.op("pe", lambda e, pv=pv, rr=rr, r=r: e.transpose(out=pv[0:NCMP, rr * 128:(rr + 1) * 128], in_=pc[:, r, 0:NCMP], identity=ident[:]),
                         reads=["pc", "ident"], writes=[pk], inc=(rr == RP - 1))
                S.op("act", lambda e, pv=pv, part=part: e.activation(
                    out=pcT[0:NCMP, part * RP:(part + 1) * RP, :], in_=pv[0:NCMP, 0:RP * 128].rearrange("p (r q) -> p r q", r=RP), func=AF.Copy),
                    reads=[pk], writes=["pcT"])
            for part in range(NPART):
                p, pk = next_ps(PM)
                S.op("pe", lambda e, p=p, part=part: e.matmul(p[:, 0:N], lhsT=vc[0:NCMP, g, :], rhs=pcT[0:NCMP, part * RP:(part + 1) * RP, :],
                                                              start=True, stop=True), reads=["vc", "pcT"], writes=[pk])
                S.op("act", lambda e, p=p, part=part: e.activation(out=ocmp[:, part * RP:(part + 1) * RP, :],
                                                                   in_=p[:, 0:N].rearrange("p (r q) -> p r q", r=RP), func=AF.Copy),
                     reads=[pk], writes=["ocmp"])
            p, pk = next_ps(PM)
            for r in range(R):
                S.op("pe", lambda e, p=p, r=r: e.matmul(p[:, 0:NSLC], lhsT=pcT[0:NCMP, r, :], rhs=mimp[0:NCMP, :], start=(r == 0), stop=(r == R - 1)),
                     reads=["pcT", "mimp"], writes=[pk], inc=(r == R - 1))
            S.op("dve", lambda e, p=p: e.tensor_tensor(out=sc[:], in0=p[:, 0:NSLC], in1=cA[:, i, :], op=ALU.mult), reads=[pk, "cA"], writes=["sc"])
            S.op("dve", lambda e: e.tensor_tensor(out=sc[:], in0=sc[:], in1=cB[:, i, :], op=ALU.add), reads=["sc", "cB"], writes=["sc"])
            S.op("dve", lambda e: e.max(out=mx1[:], in_=sc[:]), reads=["sc"], writes=["mx1"])
            S.op("dve", lambda e: e.match_replace(out=sc2[:], in_to_replace=mx1[:], in_values=sc[:], imm_value=-3e38),
                 reads=["sc", "mx1"], writes=["sc2"])
            S.op("dve", lambda e: e.max(out=mx2[:], in_=sc2[:]), reads=["sc2"], writes=["mx2"])
            S.op("dve", lambda e: e.tensor_scalar(out=sc2[:], in0=sc[:], scalar1=mx2[:, 7:8], scalar2=None, op0=ALU.is_ge),
                 reads=["sc", "mx2"], writes=["sc2"])
            S.op("dve", lambda e: e.tensor_tensor(out=selb[:], in0=sc2[:], in1=cA[:, i, :], op=ALU.mult), reads=["sc2", "cA"], writes=["selb"])
            p, pk = next_ps(PM)
            pv = p[:].bitcast(BF16)
            S.op("pe", lambda e, pv=pv: e.transpose(out=pv[0:NSLC, 0:128], in_=selb[:], identity=ident[:]), reads=["selb", "ident"], writes=[pk])
            S.op("act", lambda e, pv=pv: e.activation(out=selT[:], in_=pv[0:NSLC, 0:128], func=AF.Copy), reads=[pk], writes=["selT"])
            for kt0 in range(0, i + 1, 4):
                nk = min(4, i + 1 - kt0)
                p, pk = next_ps(PM)
                for kk in range(nk):
                    kt = kt0 + kk
                    S.op("pe", lambda e, p=p, kk=kk, kt=kt: e.matmul(p[:, kk * 128:(kk + 1) * 128], lhsT=expm[:, kt * 128:(kt + 1) * 128], rhs=selT[:],
                                                                     start=True, stop=True), reads=["expm", "selT"], writes=[pk], inc=(kk == nk - 1))
                S.op("act", lambda e, p=p, kt0=kt0, nk=nk: e.activation(out=mk[:, kt0:kt0 + nk, :],
                                                                         in_=p[:, 0:nk * 128].rearrange("p (k q) -> p k q", k=nk), func=AF.Copy),
                     reads=[pk], writes=["mk"])
            S.op("dve", lambda e: e.tensor_tensor(out=mk[:, i, :], in0=mk[:, i, :], in1=c0[:], op=ALU.mult), reads=["mk", "c0"], writes=["mk"])

            for part in range(NPART):
                hs_ = slice(part * RP, (part + 1) * RP)
                h0 = g * R + part * RP
                first_branch = [True]

                def combine(cidx, pO, pOk, pD, pDk):
                    pB, pBk = next_ps(PM)
                    for rr in range(RP):
                        row = (h0 + rr) * 3 + cidx
                        S.op("pe", lambda e, pB=pB, rr=rr, row=row: e.matmul(
                            pB[:, rr * 128:(rr + 1) * 128], lhsT=identf[0:3 * NH, row:row + 1].to_broadcast([3 * NH, 128]),
                            rhs=bgT[0:3 * NH, qc], start=True, stop=True), reads=["identf", "bgT"], writes=[pBk], inc=(rr == RP - 1))
                    pBv = pB[:, 0:N].rearrange("p (r q) -> p r q", r=RP)
                    if pD is None:
                        src, srck = pO, pOk
                    else:
                        S.op("dve", lambda e, pD=pD: e.reciprocal(out=rden[:], in_=pD[:, 0:N].rearrange("p (r q) -> p r q", r=RP)),
                             reads=[pDk], writes=["rden"])
                        S.op("dve", lambda e, pO=pO: e.tensor_tensor(out=otmp[:], in0=pO[:, 0:N].rearrange("p (r q) -> p r q", r=RP), in1=rden[:],
                                                                     op=ALU.mult), reads=[pOk, "rden"], writes=["otmp"])
                        src, srck = otmp[:], "otmp"
                    if first_branch[0]:
                        S.op("dve", lambda e, src=src, pBv=pBv: e.tensor_tensor(out=oacc[:], in0=src, in1=pBv, op=ALU.mult),
                             reads=[srck, pBk], writes=["oacc"])
                        first_branch[0] = False
                    else:
                        S.op("dve", lambda e, src=src, pBv=pBv: e.tensor_tensor(out=otmp[:], in0=src, in1=pBv, op=ALU.mult),
                             reads=[srck, pBk], writes=["otmp"])
                        S.op("dve", lambda e: e.tensor_tensor(out=oacc[:], in0=oacc[:], in1=otmp[:], op=ALU.add),
                             reads=["oacc", "otmp"], writes=["oacc"])

                combine(0, ocmp[:, hs_, :], "ocmp", None, None)
                for (cidx, kTb, kTk, vb, vk, kts, kbase) in (
                        (1, kTs, "kTs", vS, "vS", list(range(0, i + 1)), 0),
                        (2, kTw, "kTw", vW, "vW", list(range(max(0, i - 4), i + 1)), max(0, (i // 4) * 4 - 4))):
                    pO, pOk = next_ps(PA)
                    pD, pDk = next_ps(PA)
                    nk = len(kts)

                    def emit_qk(idx, kTb=kTb, kTk=kTk, kbase=kbase, kts=kts):
                        kl = kts[idx] - kbase
                        pS, pSk = next_ps(PB)
                        S.op("pe", lambda e, pS=pS, kl=kl, kTb=kTb, hs_=hs_: e.matmul(pS[:, 0:N], lhsT=kTb[:, kl * 128:(kl + 1) * 128], rhs=qT[:, hs_, qc],
                                                                                       start=True, stop=True), reads=[kTk, "qT"], writes=[pSk])
                        return pS, pSk

                    nxt = emit_qk(0)
                    for idx, kt in enumerate(kts):
                        dt_ = i - kt
                        kl = kt - kbase
                        pS, pSk = nxt
                        pSv = pS[:, 0:N].rearrange("p (r q) -> p r q", r=RP)
                        E_ = Et[idx % 2]
                        Ek = ("Et", idx % 2)
                        if dt_ <= 1:
                            S.op("dve", lambda e, pSv=pSv, dt_=dt_, hs_=hs_: e.tensor_tensor(out=Lt[:], in0=pSv, in1=TBg[:, hs_, dt_ * 128:(dt_ + 1) * 128],
                                                                                              op=ALU.add), reads=[pSk, "TBg"], writes=["Lt"])
                            S.op("act", lambda e, E_=E_: e.activation(out=E_[:], in_=Lt[:], func=AF.Exp), reads=["Lt"], writes=[Ek])
                        else:
                            S.op("act", lambda e, E_=E_, pSv=pSv: e.activation(out=E_[:], in_=pSv, func=AF.Exp), reads=[pSk], writes=[Ek])
                        if cidx == 1:
                            msk, mskk = mk[:, kt, :], "mk"
                        elif dt_ == 0:
                            msk, mskk = c0[:], "c0"
                        elif dt_ == 4:
                            msk, mskk = c4[:], "c4"
                        else:
                            msk = None
                        if msk is not None:
                            P_ = pTt[idx % 2]
                            Pk = ("pTt", idx % 2)
                            S.op("dve", lambda e, P_=P_, E_=E_, msk=msk: e.tensor_tensor(out=P_[:], in0=E_[:], in1=msk.unsqueeze(1).to_broadcast([128, RP, 128]),
                                                                                          op=ALU.mult), reads=[Ek, mskk], writes=[Pk])
                        else:
                            P_, Pk = E_, Ek
                        if idx + 1 < nk:
                            nxt = emit_qk(idx + 1)
                        S.op("pe", lambda e, pO=pO, vb=vb, kl=kl, P_=P_, idx=idx, nk=nk: e.matmul(
                            pO[:, 0:N], lhsT=vb[:, kl, :], rhs=P_[:], start=(idx == 0), stop=(idx == nk - 1)),
                            reads=[vk, Pk], writes=[pOk], inc=(idx == nk - 1))
                        S.op("pe", lambda e, pD=pD, P_=P_, idx=idx, nk=nk: e.matmul(
                            pD[:, 0:N], lhsT=onesb[:], rhs=P_[:], start=(idx == 0), stop=(idx == nk - 1)),
                            reads=["onesb", Pk], writes=[pDk], inc=(idx == nk - 1))
                    combine(cidx, pO, pOk, pD, pDk)
                S.op("dve", lambda e, part=part: e.tensor_tensor(out=act[:, g * R + part * RP:g * R + (part + 1) * RP, qc], in0=oacc[:],
                                                                 in1=sg[:, part * RP:(part + 1) * RP, qc], op=ALU.mult),
                     reads=["oacc", "sg"], writes=["actall"])


        def sample_attention():
            X = mybir.AxisListType.X
            S.op("dve", lambda e: e.tensor_copy(out=qcs[:], in_=pss[:, 300:300 + NH]), reads=["pss"], writes=["qcs"])
            S.op("act", lambda e: e.activation(out=sgs[:], in_=qcs[:], func=AF.Square), reads=["qcs"], writes=["sgs"])
            p, pk = next_ps(PM)
            S.op("pe", lambda e, p=p: e.matmul(p[:, 0:NH], lhsT=ones[:], rhs=sgs[:], start=True, stop=True), reads=["ones", "sgs"], writes=[pk])
            S.op("act", lambda e, p=p: e.activation(out=sgs[:], in_=p[:, 0:NH], func=AF.Ln, scale=1.0 / 128, bias=EPS), reads=[pk], writes=["sgs"])
            S.op("act", lambda e: e.activation(out=sgs[:], in_=sgs[:], func=AF.Exp, scale=-0.5), reads=["sgs"], writes=["sgs"])
            S.op("dve", lambda e: e.scalar_tensor_tensor(out=qns[:], in0=qcs[:], scalar=qgs[:, 0:1], in1=sgs[:], op0=ALU.mult, op1=ALU.mult),
                 reads=["qcs", "qgs", "sgs"], writes=["qns"])
            S.op("act", lambda e: e.activation(out=sgs[:], in_=pss[:, 340:340 + NH], func=AF.Silu), reads=["pss", "qns"], writes=["sgs"])
            S.op("act", lambda e: e.activation(out=bgs[0:3 * NH, :], in_=pss[0:3 * NH, 380:381], func=AF.Sigmoid, bias=gbv[0:3 * NH, 0:1]),
                 reads=["pss", "gbv"], writes=["bgs"])
            p, pk = next_ps(PM)
            for row in range(3 * NH):
                S.op("pe", lambda e, p=p, row=row: e.matmul(p[:, row:row + 1], lhsT=identf[0:3 * NH, row:row + 1].to_broadcast([3 * NH, 128]),
                                                            rhs=bgs[0:3 * NH, 0:1], start=True, stop=True),
                     reads=["identf", "bgs"], writes=[pk], inc=(row == 3 * NH - 1))
            S.op("dve", lambda e, p=p: e.tensor_copy(out=bgBs[:], in_=p[:, 0:3 * NH]), reads=[pk], writes=["bgBs"])
            for g in range(4):
                hsl = slice(g * R, (g + 1) * R)
                S.dma("sp", kcs[:], KCs[g], reads=["KCV"], writes=["kcs"])
                S.dma("sp", vcs[:], VCs[:, g * 128:(g + 1) * 128].rearrange("(n p) d -> p n d", p=128), reads=["KCV"], writes=["vcs"])
                p, pk = next_ps(PM)
                for nt in range(4):
                    S.op("pe", lambda e, p=p, nt=nt, hsl=hsl: e.matmul(p[:, nt * R:(nt + 1) * R], lhsT=kcs[:, nt * 128:(nt + 1) * 128], rhs=qns[:, hsl],
                                                              start=True, stop=True), reads=["kcs", "qns"], writes=[pk], inc=(nt == 3))
                S.op("dve", lambda e, p=p: e.tensor_copy(out=sE[:, 0:4, :], in_=p[:, 0:4 * R].rearrange("p (n r) -> p n r", n=4)), reads=[pk], writes=["sE"])
                S.op("dve", lambda e, hsl=hsl: e.tensor_tensor(out=sE[:, 3, :], in0=sE[:, 3, :], in1=sbm[:, hsl, 2], op=ALU.add), reads=["sE", "sbm"], writes=["sE"])
                S.op("act", lambda e: e.activation(out=sE[:, 0:4, :], in_=sE[:, 0:4, :], func=AF.Exp), reads=["sE"], writes=["sE"])
                S.op("dve", lambda e: e.tensor_tensor(out=sE[:, 0:4, :], in0=sE[:, 0:4, :], in1=kvalc[:, 0:4].unsqueeze(2).to_broadcast([128, 4, R]), op=ALU.mult),
                     reads=["sE", "kvalc"], writes=["sE"])
                S.op("dve", lambda e: e.tensor_reduce(out=s1[:], in_=sE[:, 0:4, :].rearrange("p n r -> p r n"), axis=X, op=ALU.add), reads=["sE"], writes=["s1"])
                p2, pk2 = next_ps(PM)
                S.op("pe", lambda e, p2=p2: e.matmul(p2[:, 0:R], lhsT=ones[:], rhs=s1[:], start=True, stop=True), reads=["ones", "s1"], writes=[pk2])
                S.op("dve", lambda e, p2=p2: e.reciprocal(out=s2[:], in_=p2[:, 0:R]), reads=[pk2], writes=["s2"])
                S.op("dve", lambda e: e.tensor_tensor(out=sP[:, 0:4, :], in0=sE[:, 0:4, :], in1=s2[:].unsqueeze(1).to_broadcast([128, 4, R]), op=ALU.mult),
                     reads=["sE", "s2"], writes=["sP"])
                p3, pk3 = next_ps(PM)
                for nt in range(4):
                    S.op("pe", lambda e, p3=p3, nt=nt: e.matmul(p3[:, 0:R], lhsT=vcs[:, nt, :], rhs=sP[:, nt, :], start=(nt == 0), stop=(nt == 3)),
                         reads=["vcs", "sP"], writes=[pk3], inc=(nt == 3))
                S.op("act", lambda e, p3=p3: e.activation(out=sO[:, 0, :], in_=p3[:, 0:R], func=AF.Copy), reads=[pk3], writes=["sO"])
                S.op("dve", lambda e: e.tensor_reduce(out=s3[:], in_=sP[:, 0:4, :], axis=X, op=ALU.add), reads=["sP"], writes=["s3"])
                S.op("dve", lambda e: e.tensor_copy(out=s3b[:], in_=s3[:]), reads=["s3"], writes=["s3b"])
                p4, pk4 = next_ps(PM)
                for nt in range(4):
                    S.op("pe", lambda e, p4=p4, nt=nt: e.matmul(p4[0:1, 0:130], lhsT=s3b[:, nt:nt + 1], rhs=mimps[:, nt, :], start=(nt == 0), stop=(nt == 3)),
                         reads=["s3b", "mimps"], writes=[pk4], inc=(nt == 3))
                S.op("dve", lambda e, p4=p4: e.tensor_tensor(out=srow[:], in0=p4[0:1, 0:130], in1=cbs[:], op=ALU.add), reads=[pk4, "cbs"], writes=["srow"])
                S.op("dve", lambda e: e.max(out=smx[0:1, 0:8], in_=srow[:]), reads=["srow"], writes=["smx"])
                S.op("dve", lambda e: e.match_replace(out=srow2[:], in_to_replace=smx[0:1, 0:8], in_values=srow[:], imm_value=-3e38),
                     reads=["srow", "smx"], writes=["srow2"])
                S.op("dve", lambda e: e.max(out=smx[0:1, 8:16], in_=srow2[:]), reads=["srow2"], writes=["smx"])
                S.op("dve", lambda e: e.tensor_scalar(out=srow2[:], in0=srow[:], scalar1=smx[0:1, 15:16], scalar2=None, op0=ALU.is_ge),
                     reads=["srow", "smx"], writes=["srow2"])
                p5, pk5 = next_ps(PM)
                S.op("pe", lambda e, p5=p5: e.matmul(p5[:, 0:130], lhsT=ones[0:1, :], rhs=srow2[:], start=True, stop=True), reads=["ones", "srow2"], writes=[pk5])
                S.op("dve", lambda e, p5=p5: e.tensor_tensor(out=MKs[0:64, :], in0=p5[0:64, 0:130:2], in1=kval[0:64, :], op=ALU.mult),
                     reads=[pk5, "kval"], writes=["MKs"])
                S.op("dve", lambda e, p5=p5: e.tensor_tensor(out=MKs[64:128, :], in0=p5[64:128, 1:130:2], in1=kval[64:128, :], op=ALU.mult),
                     reads=[pk5, "kval"], writes=["MKs"])
                for (cidx, KT_, V_, ntile, kTb, kTk, vb, vk, cw) in ((1, KSs, VSs, 65, kTs, "kTs", vS, "vS", 16), (2, KWs, VWs, 5, kTw, "kTw", vW, "vW", 8)):
                    pO, pOk = next_ps(PA)
                    pD, pDk = next_ps(PA)
                    for k0 in range(0, ntile, cw):
                        nkt = min(cw, ntile - k0)
                        S.dma("sp", kTb[:, 0:nkt * 128], KT_[g][:, k0 * 128:(k0 + nkt) * 128], reads=["SKV"], writes=[kTk])
                        S.dma("sp", vb[:, 0:nkt, :], V_[k0 * 128:(k0 + nkt) * 128, g * 128:(g + 1) * 128].rearrange("(k p) d -> p k d", p=128),
                              reads=["SKV"], writes=[vk])
                        pS, pSk = next_ps(PB)
                        for kl in range(nkt):
                            S.op("pe", lambda e, pS=pS, kl=kl, kTb=kTb, hsl=hsl: e.matmul(pS[:, kl * R:(kl + 1) * R], lhsT=kTb[:, kl * 128:(kl + 1) * 128], rhs=qns[:, hsl],
                                                                                  start=True, stop=True), reads=[kTk, "qns"], writes=[pSk], inc=(kl == nkt - 1))
                        S.op("dve", lambda e, pS=pS, nkt=nkt: e.tensor_copy(out=sE[:, 0:nkt, :], in_=pS[:, 0:nkt * R].rearrange("p (k r) -> p k r", k=nkt)),
                             reads=[pSk], writes=["sE"])
                        for (tile_, col) in ((ntile - 2, 0), (ntile - 1, 1)):
                            if k0 <= tile_ < k0 + nkt:
                                S.op("dve", lambda e, kl=tile_ - k0, col=col, hsl=hsl: e.tensor_tensor(out=sE[:, kl, :], in0=sE[:, kl, :], in1=sbm[:, hsl, col], op=ALU.add),
                                     reads=["sE", "sbm"], writes=["sE"])
                        S.op("act", lambda e, nkt=nkt: e.activation(out=sEb[:, 0:nkt, :], in_=sE[:, 0:nkt, :], func=AF.Exp), reads=["sE"], writes=["sEb"])
                        if cidx == 1:
                            mka, mkk = MKs[:, k0:k0 + nkt], "MKs"
                            S.op("dve", lambda e, nkt=nkt, mka=mka: e.tensor_tensor(out=sP[:, 0:nkt, :], in0=sEb[:, 0:nkt, :],
                                                                                     in1=mka.unsqueeze(2).to_broadcast([128, nkt, R]), op=ALU.mult),
                                 reads=["sEb", mkk], writes=["sP"])
                        else:
                            S.op("dve", lambda e, nkt=nkt: e.tensor_tensor(out=sP[:, 0:nkt, :], in0=sEb[:, 0:nkt, :],
                                                                            in1=kval[:, 60:65].unsqueeze(2).to_broadcast([128, nkt, R]), op=ALU.mult),
                                 reads=["sEb", "kval"], writes=["sP"])
                        for kl in range(nkt):
                            tl = k0 + kl
                            S.op("pe", lambda e, pO=pO, kl=kl, vb=vb, tl=tl, ntile=ntile: e.matmul(pO[:, 0:R], lhsT=vb[:, kl, :], rhs=sP[:, kl, :],
                                                                                                    start=(tl == 0), stop=(tl == ntile - 1)),
                                 reads=[vk, "sP"], writes=[pOk], inc=False)
                            S.op("pe", lambda e, pD=pD, kl=kl, tl=tl, ntile=ntile: e.matmul(pD[:, 0:R], lhsT=onesb[:], rhs=sP[:, kl, :],
                                                                                             start=(tl == 0), stop=(tl == ntile - 1)),
                                 reads=["onesb", "sP"], writes=[pDk], inc=(kl == nkt - 1))
                    S.op("dve", lambda e, pD=pD: e.reciprocal(out=s2[:], in_=pD[:, 0:R]), reads=[pDk], writes=["s2"])
                    S.op("dve", lambda e, pO=pO, cidx=cidx: e.tensor_tensor(out=sO[:, cidx, :], in0=pO[:, 0:R], in1=s2[:], op=ALU.mult),
                         reads=[pOk, "s2"], writes=["sO"])
                S.op("dve", lambda e, g=g: e.tensor_tensor(out=sOg[:], in0=sO[:], in1=bgBs[:, g * R * 3:(g + 1) * R * 3].rearrange("p (r c) -> p c r", c=3), op=ALU.mult),
                     reads=["sO", "bgBs"], writes=["sOg"])
                S.op("dve", lambda e: e.tensor_reduce(out=s1[:], in_=sOg[:].rearrange("p c r -> p r c"), axis=X, op=ALU.add), reads=["sOg"], writes=["s1"])
                S.op("dve", lambda e, hsl=hsl: e.tensor_tensor(out=acts[:, hsl], in0=s1[:], in1=sgs[:, hsl], op=ALU.mult), reads=["s1", "sgs"], writes=["acts"])

        for li in range(2):
            Xout = XR if li == 0 else o_yp
            S.dma("sp", bnv[:], b_norm_in[li], writes=["bnv"])
            S.dma("sp", qgs[:], b_qn[li], writes=["qgs"])
            S.op("dve", lambda e: e.tensor_scalar(out=qgs[:], in0=qgs[:], scalar1=128.0 ** -0.5, scalar2=None, op0=ALU.mult),
                 reads=["qgs"], writes=["qgs"])
            S.dma("sp", gbv[0:3 * NH, :], b_gb[li], writes=["gbv"])
            S.dma("pool", wbg[:], b_w_in[li][:, 2 * D:2 * D + 3 * NH].rearrange("(c p) n -> p c n", p=128), writes=["wbg"])
            make_xns(bnv[:], "bnv")
            for tt in range(NT):
                last = tt == NT - 1
                make_xnT(XR, tt, bnv[:], "bnv")
                p, pk = next_ps()
                for dch in range(NCH):
                    S.op("pe", lambda e, p=p, dch=dch: e.matmul(p[0:3 * NH, :], lhsT=wbg[:, dch, :], rhs=xnT[:, dch, :],
                                                                start=(dch == 0), stop=(dch == NCH - 1)), reads=["wbg", "xnT"], writes=[pk], inc=(dch == NCH - 1))
                S.op("act", lambda e, p=p: e.activation(out=bgT[0:3 * NH, :], in_=p[0:3 * NH, :], func=AF.Sigmoid, bias=gbv[0:3 * NH, 0:1]),
                     reads=[pk, "gbv"], writes=["bgT"])
                if last:
                    for dch in range(NCH):
                        S.op("pe", lambda e, dch=dch: e.matmul(pss[0:3 * NH, 380:381], lhsT=wbg[:, dch, :], rhs=xns[:, dch:dch + 1],
                                                               start=(dch == 0), stop=(dch == NCH - 1)), reads=["wbg", "xns"], writes=["pss"], inc=(dch == NCH - 1))
                nk_s = (tt + 1) * 4
                kb_w = max(0, tt * 4 - 4)
                for g in range(4):
                    S.dma("sp", TBg[:], TBM[:, g * R:(g + 1) * R, :], reads=["BIAS"], writes=["TBg"])
                    S.dma("sp", CBg[:], CBM[:, g * R:(g + 1) * R, :], reads=["BIAS"], writes=["CBg"])
                    S.dma("sp", kTs[:, 0:nk_s * 128], KST[g][:, 0:nk_s * 128], reads=["KVT"], writes=["kTs"])
                    S.dma("sp", vS[:, 0:nk_s, :], VS[0:nk_s * 128, g * 128:(g + 1) * 128].rearrange("(k p) n -> p k n", p=128), writes=["vS"])
                    S.dma("sp", kTw[:, 0:(nk_s - kb_w) * 128], KWT[g][:, kb_w * 128:nk_s * 128], reads=["KVT"], writes=["kTw"])
                    S.dma("sp", vW[:, 0:nk_s - kb_w, :], VW[kb_w * 128:nk_s * 128, g * 128:(g + 1) * 128].rearrange("(k p) n -> p k n", p=128),
                          writes=["vW"])
                    for r in range(R):
                        h = g * R + r
                        w, wk = load_wD(b_w_in[li][:, h * 128:(h + 1) * 128])
                        p, pk = next_ps()
                        for dch in range(NCH):
                            S.op("pe", lambda e, p=p, w=w, dch=dch: e.matmul(p[:], lhsT=w[:, dch, :], rhs=xnT[:, dch, :],
                                                                             start=(dch == 0), stop=(dch == NCH - 1)),
                                 reads=[wk, "xnT"], writes=[pk], inc=(dch == NCH - 1))
                        if last:
                            for dch in range(NCH):
                                S.op("pe", lambda e, w=w, dch=dch, h=h: e.matmul(pss[:, 300 + h:301 + h], lhsT=w[:, dch, :], rhs=xns[:, dch:dch + 1],
                                                                                 start=(dch == 0), stop=(dch == NCH - 1)), reads=[wk, "xns"], writes=["pss"], inc=(dch == NCH - 1))
                        S.op("act", lambda e, p=p: e.activation(out=qsq[:], in_=p[:], func=AF.Square), reads=[pk], writes=["qsq"])
                        p2, pk2 = next_ps()
                        S.op("pe", lambda e, p2=p2: e.matmul(p2[:], lhsT=onesb[:], rhs=qsq[:], start=True, stop=True),
                             reads=["onesb", "qsq"], writes=[pk2])
                        S.op("act", lambda e, p2=p2: e.activation(out=rst[:], in_=p2[:], func=AF.Ln, scale=1.0 / 128, bias=EPS), reads=[pk2], writes=["rst"])
                        S.op("act", lambda e: e.activation(out=rst[:], in_=rst[:], func=AF.Exp, scale=-0.5), reads=["rst"], writes=["rst"])
                        S.op("dve", lambda e, p=p, r=r: e.scalar_tensor_tensor(out=qT[:, r, :], in0=p[:], scalar=qgs[:, 0:1], in1=rst[:],
                                                                                op0=ALU.mult, op1=ALU.mult), reads=[pk, "qgs", "rst"], writes=["qT"])
                        w, wk = load_wD(b_w_in[li][:, D + h * 128:D + (h + 1) * 128])
                        p3, pk3 = next_ps()
                        for dch in range(NCH):
                            S.op("pe", lambda e, p3=p3, w=w, dch=dch: e.matmul(p3[:], lhsT=w[:, dch, :], rhs=xnT[:, dch, :],
                                                                               start=(dch == 0), stop=(dch == NCH - 1)),
                                 reads=[wk, "xnT"], writes=[pk3], inc=(dch == NCH - 1))
                        if last:
                            for dch in range(NCH):
                                S.op("pe", lambda e, w=w, dch=dch, h=h: e.matmul(pss[:, 340 + h:341 + h], lhsT=w[:, dch, :], rhs=xns[:, dch:dch + 1],
                                                                                 start=(dch == 0), stop=(dch == NCH - 1)), reads=[wk, "xns"], writes=["pss"], inc=(dch == NCH - 1))
                        S.op("act", lambda e, p3=p3, r=r: e.activation(out=sg[:, r, :], in_=p3[:], func=AF.Silu), reads=[pk3], writes=["sg"])
                    for qt in range(4):
                        attention(g, tt * 4 + qt, qt)
                if last:
                    sample_attention()
                    if DEBUG_DUMP and li == 0:
                        for nm, tl, sh, dt_ in (("d_acts", acts, [128, NCH], BF16), ("d_qns", qns, [128, NH], BF16), ("d_sgs", sgs, [128, NH], F32),
                                                ("d_bgBs", bgBs, [128, 3 * NH], F32), ("d_sO", sO, [128, 3, R], F32), ("d_MKs", MKs, [128, 65], BF16),
                                                ("d_srow", srow, [1, 130], F32), ("d_qcs", qcs, [128, NH], F32)):
                            dd = nc.dram_tensor(nm, sh, dt_).ap()
                            S.dma("sp", dd, tl[:], reads=["acts", "qns", "sgs", "bgBs", "sO", "MKs", "srow", "qcs"])
                rows = slice(tt * TT, (tt + 1) * TT)
                for dc in range(NCH):
                    w, wk = load_wD(b_w_out[li][:, dc * 128:(dc + 1) * 128])
                    if last:
                        for ch in range(NCH):
                            S.op("pe", lambda e, w=w, ch=ch, dc=dc: e.matmul(pss[:, 400 + dc:401 + dc], lhsT=w[:, ch, :], rhs=acts[:, ch:ch + 1],
                                                                             start=(ch == 0), stop=(ch == NCH - 1)), reads=[wk, "acts"], writes=["pss"], inc=(ch == NCH - 1))
                    xr_t = xres[dc % 2]
                    xrk = ("xres", dc % 2)
                    S.dma("sp", xr_t[:, :, 0:128], XR[rows, dc * 128:(dc + 1) * 128].rearrange("(s p) n -> p s n", p=128),
                          reads=["XR"], writes=[xrk])
                    for sub in range(4):
                        p, pk = next_ps()
                        for ch in range(NCH):
                            S.op("pe", lambda e, p=p, w=w, ch=ch, sub=sub: e.matmul(
                                p[:, 0:128], lhsT=act[:, ch, sub * 128:(sub + 1) * 128], rhs=w[:, ch, :],
                                start=(ch == 0), stop=(ch == NCH - 1)), reads=[wk, "actall"], writes=[pk], inc=(ch == NCH - 1))
                        S.op("dve", lambda e, p=p, xr_t=xr_t, sub=sub: e.tensor_tensor(
                            out=xr_t[:, sub, 0:128], in0=p[:, 0:128], in1=xr_t[:, sub, 0:128], op=ALU.add), reads=[pk, xrk], writes=[xrk])
                    S.dma("sp", Xout[rows, dc * 128:(dc + 1) * 128].rearrange("(s p) n -> p s n", p=128), xr_t[:, :, 0:128],
                          reads=[xrk], writes=["XR"])
                if last:
                    S.op("dve", lambda e: e.tensor_tensor(out=xs[:], in0=xs[:], in1=pss[:, 400:400 + NCH], op=ALU.add), reads=["xs", "pss"], writes=["xs"])
        S.dma("sp", o_ys, xs[:], reads=["xs"])
        S.flush()
        esD.close()

    try:
        rglru_layer(0)
        ckpt(10)
        rglru_layer(1)
        ckpt(11)
        kv_proj()
        ckpt(12)
    except _Stop:
        pass
    S.flush()
    esA.close()
    try:
        ckpt(12)
        bias_phase()
        ckpt(13)
        compress_phase(False)
        ckpt(14)
        sample_prep()
        ckpt(15)
        compress_phase(True)
        ckpt(16)
        nsa_phase()
    except _Stop:
        pass
    S.dma("sp", o_ys, xs[:], reads=["xs"])
    S.finalize()
    es.close()
    return nc


_NC = None


def fm(v):
    v = np.asarray(v)
    return np.ascontiguousarray(np.swapaxes(v.reshape(v.shape[:-1] + (NCH, 128)), -1, -2))


def unfm(a):
    return np.ascontiguousarray(np.swapaxes(a, -1, -2)).reshape(a.shape[:-2] + (D,))


def host_consts(T_):
    f32 = np.float32
    nqt, nslc, ncmp = T_ // 128, T_ // 64, (T_ - 32) // 16 + 1
    n = np.arange(128)[:, None]
    j = np.arange(nslc)[None, :]
    ov = np.minimum(n * 16 + 32, j * 64 + 64) - np.maximum(n * 16, j * 64)
    mimp = (np.maximum(ov, 0) / 32.0).astype(f32)
    mimp[ncmp:] = 0
    keys = np.arange(T_)[None, :]
    expm = (keys // 64 == np.arange(nslc)[:, None]).astype(f32)
    k = np.arange(128)[:, None]
    q = np.arange(128)[None, :]
    c0 = (q >= k).astype(f32)
    c4 = (q <= k).astype(f32)
    ql = np.arange(128)[:, None, None]
    i = np.arange(nqt)[None, :, None]
    jj = np.arange(nslc)[None, None, :]
    cur = 2 * i + (ql >= 64)
    allowed = jj <= cur
    forced = (jj == 0) | (jj == cur) | (jj == cur - 1)
    cA = allowed.astype(f32)
    cB = np.where(forced, 1e30 + jj * 1e26, np.where(allowed, 0.0, -1e30 - jj * 1e26)).astype(f32)
    ns = (np.arange(4)[None, :, None] * 128 + np.arange(128)[:, None, None])
    js = np.arange(130)[None, None, :]
    ovs = np.minimum(ns * 16 + 32, js * 64 + 64) - np.maximum(ns * 16, js * 64)
    mimps = (np.maximum(ovs, 0) / 32.0).astype(f32) * (ns < 511) * (js < 129)
    kval = np.ones((128, 65), f32)
    kval[1:, 64] = 0
    kvalc = np.ones((128, 4), f32)
    kvalc[127, 3] = 0
    j1 = np.arange(130)
    cbs = np.where((j1 == 0) | (j1 == 127) | (j1 == 128), 1e30 + j1 * 1e26, 0.0)
    cbs[129] = -1e30
    return dict(c_mimp=mimp, c_expm=expm, c_c0=c0, c_c4=c4, c_A=np.ascontiguousarray(cA), c_B=np.ascontiguousarray(cB),
                c_mimps=np.ascontiguousarray(mimps.astype(f32)), c_kval=kval, c_kvalc=kvalc, c_cbs=cbs.astype(f32)[None, :])


def kernel(x_prompt, x_sample, cache_cmp_kv, cache_slc_kv, state_win_kv, state_lru_h,
           state_conv, page_table, a_norm, a_w_in, a_conv_w, a_conv_b, a_w_rg, a_b_rg,
           a_w_ig, a_b_ig, a_lambda, a_w_out, kv_norm, w_kv, k_norm, cmp_pos, w_cmp1,
           w_cmp2, rel_table, b_norm, b_w_in, b_gate_bias, b_q_norm, b_w_out):
    global _NC
    if _NC is None:
        _NC = build()
    nc = _NC
    f32 = np.float32
    A = np.asarray
    vecs = np.stack([A(a_norm), A(a_conv_w)[:, 0], A(a_conv_w)[:, 1], A(a_conv_w)[:, 2], A(a_conv_w)[:, 3],
                     A(a_conv_b), A(a_b_rg), A(a_b_ig), A(a_lambda)], axis=1)
    avec = np.ascontiguousarray(fm(vecs).transpose(0, 2, 1, 3)).astype(f32)
    kvn_fm = fm(A(kv_norm)).astype(f32)
    ident = np.eye(128, dtype=f32)
    common = dict(a_w_in=A(a_w_in), a_w_out=A(a_w_out), a_w_rg=A(a_w_rg), a_w_ig=A(a_w_ig), avec=avec,
                  kvn_fm=kvn_fm, w_kv=A(w_kv), ident=ident,
                  k_norm=A(k_norm).astype(f32), rel_table=A(rel_table).astype(f32),
                  cmp_posT=np.ascontiguousarray(A(cmp_pos).transpose(2, 0, 1)).astype(f32),
                  w_cmp1=A(w_cmp1), w_cmp2=A(w_cmp2), b_norm_fm=fm(A(b_norm)).astype(f32), b_w_in=A(b_w_in),
                  b_gb=np.ascontiguousarray(A(b_gate_bias)[..., None]).astype(f32),
                  b_qn=np.ascontiguousarray(A(b_q_norm)[..., None]).astype(f32), b_w_out=A(b_w_out))
    common.update(host_consts(T))
    common["cache_cmp"] = A(cache_cmp_kv).reshape(640 * 128, 1024)
    common["cache_slc"] = A(cache_slc_kv).reshape(640 * 128, 1024)
    in_maps = []
    for c in range(8):
        s = c // 2
        m = dict(common)
        m["xp"] = np.ascontiguousarray(A(x_prompt)[s])
        m["xs_fm"] = fm(A(x_sample)[c, 0]).astype(f32)
        m["st_h"] = fm(A(state_lru_h)[:, c]).astype(f32)
        m["st_conv"] = np.ascontiguousarray(fm(A(state_conv)[:, c]).transpose(0, 2, 3, 1)).astype(f32)
        m["st_win"] = np.ascontiguousarray(A(state_win_kv)[c].reshape(512, 1024))
        m["ptab"] = np.ascontiguousarray(A(page_table)[c:c + 1]).astype(np.int32)
        in_maps.append(m)
    res = run_bass_kernel_spmd(nc, in_maps, core_ids=list(range(8))).results
    R = lambda c, k: np.asarray(res[c][k])
    y_prompt = np.stack([R(2 * s, "o_yp") for s in range(4)]).reshape(4, T, D)
    y_sample = np.stack([unfm(R(c, "o_ys")) for c in range(8)]).reshape(8, 1, D)
    p_cmp = np.stack([R(2 * s, "o_pcmp") for s in range(4)]).reshape(4, T, 2, 4, 128)
    p_slc = np.stack([R(2 * s, "o_pslc") for s in range(4)]).reshape(4, T, 2, 4, 128)
    p_win = np.stack([R(2 * s, "o_pwin") for s in range(4)]).reshape(4, 512, 2, 4, 128)
    p_lru_h = np.stack([unfm(R(2 * s, "o_plh")) for s in range(4)], axis=1)
    p_conv = np.stack([unfm(R(2 * s, "o_pconv").transpose(0, 3, 1, 2)) for s in range(4)], axis=1)
    s_cmp = np.stack([R(c, "o_scmp") for c in range(8)]).reshape(8, 1, 2, 4, 128)
    s_slc = np.stack([R(c, "o_sslc") for c in range(8)]).reshape(8, 1, 2, 4, 128)
    s_win = np.stack([R(c, "o_swin") for c in range(8)]).reshape(8, 512, 2, 4, 128)
    s_lru_h = np.stack([unfm(R(c, "o_slh")) for c in range(8)], axis=1)
    s_conv = np.stack([unfm(R(c, "o_sconv").transpose(0, 3, 1, 2)) for c in range(8)], axis=1)
    outs = (y_prompt, y_sample, p_cmp, p_slc, p_win, p_lru_h, p_conv, s_cmp, s_slc, s_win, s_lru_h, s_conv)
    return tuple(np.ascontiguousarray(o, dtype=f32) for o in outs)
```
